# Optimizing a Trainium2 kernel written in Bass

```python
import jax, jax.numpy as jnp
from jax import lax
import numpy as np

D_MODEL = 1024
BATCH = 4
SEQ = 4096
DEPTH = 1

CHUNK = 64
LEFT_CHUNKS = 8
BAND = (LEFT_CHUNKS + 1) * CHUNK
D_MIX = D_MODEL
D_ATTN = D_MIX // 2
HEAD_DIM = 64
N_HEADS = D_ATTN // HEAD_DIM
D_CONV = D_MIX - D_ATTN
CONV_GROUPS = 8
CONV_WIDTH = 31
MAX_REL = 128
D_FF = 2816
D_IN = 3 * D_ATTN + 2 * D_CONV
EPS = 1e-6
NEG_INF = -1e30

kernel_name = "hymba_conformer_chunk_attn_conv_block"


def rmsnorm(x, g):
    x32 = x.astype(jnp.float32)
    r = x32 * lax.rsqrt(jnp.mean(x32 * x32, axis=-1, keepdims=True) + EPS)
    return (r * g.astype(jnp.float32)).astype(x.dtype)


def layernorm(x, g, b):
    x32 = x.astype(jnp.float32)
    mu = jnp.mean(x32, axis=-1, keepdims=True)
    var = jnp.mean(jnp.square(x32 - mu), axis=-1, keepdims=True)
    r = (x32 - mu) * lax.rsqrt(var + EPS)
    return (r * g.astype(jnp.float32) + b.astype(jnp.float32)).astype(x.dtype)


def swiglu(h, w_gate, w_up, w_down):
    return (jax.nn.silu(h @ w_gate) * (h @ w_up)) @ w_down


def key_band(t, n_chunks):
    b, _, h, d = t.shape
    tc = t.reshape(b, n_chunks, CHUNK, h, d)
    tp = jnp.pad(tc, ((0, 0), (LEFT_CHUNKS, 0), (0, 0), (0, 0), (0, 0)))
    return jnp.concatenate([tp[:, j:j + n_chunks] for j in range(LEFT_CHUNKS + 1)], axis=2)


def chunk_band_attention(q, k, v, rel_bias):
    b, t, h, d = q.shape
    n_chunks = t // CHUNK
    qc = q.reshape(b, n_chunks, CHUNK, h, d)
    kb = key_band(k, n_chunks)
    vb = key_band(v, n_chunks)
    qi = jnp.arange(CHUNK)[:, None]
    kj = jnp.arange(BAND)[None, :]
    idx = jnp.clip(qi + LEFT_CHUNKS * CHUNK - kj, -MAX_REL, MAX_REL) + MAX_REL
    bias = jnp.transpose(rel_bias[idx], (2, 0, 1)).astype(jnp.float32)
    key_pos = (jnp.arange(n_chunks)[:, None] - LEFT_CHUNKS) * CHUNK + kj
    valid = key_pos >= 0
    scale = 1.0 / np.sqrt(HEAD_DIM)
    s = jnp.einsum('bcqhd,bckhd->bchqk', qc, kb).astype(jnp.float32) * scale
    s = s + bias[None, None]
    s = jnp.where(valid[None, :, None, None, :], s, NEG_INF)
    p = jax.nn.softmax(s, axis=-1).astype(v.dtype)
    o = jnp.einsum('bchqk,bckhd->bcqhd', p, vb)
    return o.reshape(b, t, h * d)


def conformer_conv(u, dw_kernel, dw_bias, ln_g, ln_b):
    g = u[..., :D_CONV] * jax.nn.sigmoid(u[..., D_CONV:])
    dw = lax.conv_general_dilated(
        g, dw_kernel[:, None, :].astype(g.dtype), window_strides=(1,),
        padding=[(CONV_WIDTH - 1, 0)],
        dimension_numbers=('NWC', 'WIO', 'NWC'),
        feature_group_count=D_CONV) + dw_bias
    return jax.nn.silu(layernorm(dw, ln_g, ln_b))


def setup_inputs(seed: int = 0) -> dict:
    key = jax.random.key(seed)
    ks = jax.random.split(key, 20)
    L = DEPTH
    f32 = jnp.float32

    def normal(k, shape, scale):
        return jax.random.normal(k, shape, f32) * scale

    def gain(k, shape):
        return 1.0 + 0.02 * jax.random.normal(k, shape, f32)

    return {
        "x": jax.random.normal(ks[0], (BATCH, SEQ, D_MODEL), f32),
        "ffn1_norm": gain(ks[1], (L, D_MODEL)),
        "ffn1_gate": normal(ks[2], (L, D_MODEL, D_FF), D_MODEL ** -0.5),
        "ffn1_up": normal(ks[3], (L, D_MODEL, D_FF), D_MODEL ** -0.5),
        "ffn1_down": normal(ks[4], (L, D_FF, D_MODEL), D_FF ** -0.5),
        "mix_norm": gain(ks[5], (L, D_MODEL)),
        "w_in": normal(ks[6], (L, D_MODEL, D_IN), D_MODEL ** -0.5),
        "rel_bias": normal(ks[7], (L, 2 * MAX_REL + 1, N_HEADS), 0.1),
        "dw_kernel": normal(ks[8], (L, CONV_WIDTH, D_CONV), CONV_WIDTH ** -0.5),
        "dw_bias": normal(ks[9], (L, D_CONV), 0.02),
        "conv_ln_g": gain(ks[10], (L, D_CONV)),
        "conv_ln_b": normal(ks[11], (L, D_CONV), 0.02),
        "w_out": normal(ks[12], (L, D_MIX, D_MODEL), D_MIX ** -0.5),
        "ffn2_norm": gain(ks[13], (L, D_MODEL)),
        "ffn2_gate": normal(ks[14], (L, D_MODEL, D_FF), D_MODEL ** -0.5),
        "ffn2_up": normal(ks[15], (L, D_MODEL, D_FF), D_MODEL ** -0.5),
        "ffn2_down": normal(ks[16], (L, D_FF, D_MODEL), D_FF ** -0.5),
        "final_norm": gain(ks[17], (D_MODEL,)),
    }


def reference(x, ffn1_norm, ffn1_gate, ffn1_up, ffn1_down, mix_norm, w_in, rel_bias,
              dw_kernel, dw_bias, conv_ln_g, conv_ln_b, w_out, ffn2_norm, ffn2_gate,
              ffn2_up, ffn2_down, final_norm):
    b, t, _ = x.shape
    for l in range(DEPTH):
        x = x + 0.5 * swiglu(rmsnorm(x, ffn1_norm[l]), ffn1_gate[l], ffn1_up[l], ffn1_down[l])
        z = rmsnorm(x, mix_norm[l]) @ w_in[l]
        q = z[..., 0 * D_ATTN:1 * D_ATTN].reshape(b, t, N_HEADS, HEAD_DIM)
        k = z[..., 1 * D_ATTN:2 * D_ATTN].reshape(b, t, N_HEADS, HEAD_DIM)
        v = z[..., 2 * D_ATTN:3 * D_ATTN].reshape(b, t, N_HEADS, HEAD_DIM)
        u = z[..., 3 * D_ATTN:]
        attn_out = chunk_band_attention(q, k, v, rel_bias[l])
        conv_out = conformer_conv(u, dw_kernel[l], dw_bias[l], conv_ln_g[l], conv_ln_b[l])
        x = x + jnp.concatenate([attn_out, conv_out], axis=-1) @ w_out[l]
        x = x + 0.5 * swiglu(rmsnorm(x, ffn2_norm[l]), ffn2_gate[l], ffn2_up[l], ffn2_down[l])
    return rmsnorm(x, final_norm)
```

```python
import numpy as np
from contextlib import ExitStack

import concourse.bass as bass
import concourse.mybir as mybir
from concourse.bass_utils import run_bass_kernel_spmd

F32 = mybir.dt.float32
BF16 = mybir.dt.bfloat16
AF = mybir.ActivationFunctionType
ALU = mybir.AluOpType

D = 1024
DFF = 2816
T = 1280
SUBS = [(0, 512), (512, 512), (1024, 256)]
EPS = 1e-6
NEG = -30000.0
HALF = [(0, 12), (12, 10)]

CS_GAIN = 0
CS_DWK = 32
CS_DWB = 156
CS_LNG = 160
CS_LNB = 164
CS_RB = 168
CS_N = 176
CB_BG = 0
CB_M4 = 2048
CB_M0 = 2176
CB_HV = 2688
CB_ID = 2816
CB_N = 2944
B_BT = 0
B_M0 = 2048
B_HV = 2560
B_ID = 2688
B_ONE = 2816
B_ONES = 2944
B_ONE5 = 3072
B_N = 3200


class _Eng:
    def __init__(self, name):
        self.name = name
        self.items = []
        self.count = 0
        self.waited = {}


class Prog:
    ENGS = ("pe", "act", "dve", "pool", "sp")

    def __init__(self):
        self.E = {n: _Eng(n) for n in self.ENGS}
        self.lastw = {}
        self.readers = {}
        self.dma_cnt = {}

    def _need(self, e, tok):
        sem, val, _ = tok
        E = self.E[e]
        if E.waited.get(sem, 0) >= val:
            return
        E.waited[sem] = val
        E.items.append(("wait", sem, val))

    def _deps(self, e, reads, writes):
        for k in reads:
            t = self.lastw.get(k)
            if t is not None:
                if t[2] == e and e == "pe":
                    continue
                self._need(e, t)
        for k in writes:
            t = self.lastw.get(k)
            if t is not None and t[2] != e:
                self._need(e, t)
            for t in self.readers.get(k, {}).values():
                if t[2] != e:
                    self._need(e, t)

    def _record(self, tok, reads, writes):
        for k in reads:
            self.readers.setdefault(k, {})[tok[2]] = tok
        for k in writes:
            self.lastw[k] = tok
            self.readers[k] = {}

    def op(self, e, fn, reads=(), writes=(), inc=True):
        E = self.E[e]
        self._deps(e, reads, writes)
        if inc:
            E.count += 1
            tok = ("e_" + e, E.count, e)
        else:
            tok = ("e_" + e, E.count + 1, e)
        E.items.append(("op", fn, inc))
        self._record(tok, reads, writes)
        return tok

    def dma(self, e, fn, semname, reads=(), writes=()):
        E = self.E[e]
        self._deps(e, reads, writes)
        c = self.dma_cnt.get(semname, 0) + 1
        self.dma_cnt[semname] = c
        tok = (semname, 16 * c, "dma:" + semname)
        E.items.append(("dma", fn, semname))
        self._record(tok, reads, writes)
        return tok

    def barrier(self, engs=("pe", "act", "dve")):
        toks = {e: ("e_" + e, self.E[e].count, e) for e in engs if self.E[e].count > 0}
        for e in engs:
            for f, t in toks.items():
                if f != e:
                    self._need(e, t)

    def compress(self):
        waited = {e: set() for e in self.ENGS}
        for E in self.E.values():
            for it in E.items:
                if it[0] == "wait" and it[1].startswith("e_"):
                    waited[it[1][2:]].add(it[2])
        remap = {}
        for e, E in self.E.items():
            vals = sorted(waited[e])
            remap["e_" + e] = {v: i + 1 for i, v in enumerate(vals)}
            cnt = 0
            new = []
            for it in E.items:
                if it[0] == "op" and it[2]:
                    cnt += 1
                    new.append(("op", it[1], cnt in waited[e]))
                else:
                    new.append(it)
            E.items = new
        for E in self.E.values():
            E.items = [("wait", it[1], remap[it[1]][it[2]]) if (it[0] == "wait" and it[1] in remap) else it
                       for it in E.items]

    def sem_names(self):
        s = ["e_" + e for e in self.ENGS]
        s += list(self.dma_cnt.keys())
        return s


def _replay(h, E, sems):
    for it in E.items:
        if it[0] == "wait":
            h.wait_ge(sems[it[1]], it[2])
        elif it[0] == "op":
            ins = it[1](h)
            if it[2]:
                ins.then_inc(sems["e_" + E.name], 1)
        else:
            it[1](h).then_inc(sems[it[2]], 16)


def build_nc(debug=None):
    nc = bass.Bass("TRN2", target_bir_lowering=False)
    P = Prog()
    _NC_CACHE["prog"] = P

    def din(name, shape):
        return nc.dram_tensor(name, list(shape), F32, kind="ExternalInput").ap()

    xin_d = din("xin", [2, 128, 8 * T])
    wgu_d = [din("wgu%d" % i, [11, 128, 4096]) for i in range(2)]
    wdA_d = [din("wdA%d" % i, [4, 128, 3072]) for i in range(2)]
    wdB_d = [din("wdB%d" % i, [4, 128, 2560]) for i in range(2)]
    win_d = din("win", [5, 128, 4096])
    wout_d = din("wout", [2, 128, 4096])
    cs_d = din("cs", [128, CS_N])
    cbig_d = din("cbig", [128, CB_N])
    yout_d = nc.dram_tensor("yout", [128, 8 * 2048], F32, kind="ExternalOutput").ap()
    dbg_d = None
    if debug:
        dbg_d = nc.dram_tensor("dbg", [128, 8 * T], F32, kind="ExternalOutput").ap()

    with ExitStack() as es:
        def sb(name, shape, dt):
            return es.enter_context(nc.sbuf_tensor(name, list(shape), dt))

        xres_f = sb("xres", [128, 8 * T], F32)
        XOFF = [0, 8 * 512, 8 * 1024]

        def xblk(s_):
            n_ = SUBS[s_][1]
            return xres_f[:, XOFF[s_]:XOFF[s_] + 8 * n_]

        def xsl(kc, s_):
            n_ = SUBS[s_][1]
            return xres_f[:, XOFF[s_] + kc * n_:XOFF[s_] + (kc + 1) * n_]
        gur = sb("gur", [128, 3, 4096], BF16)
        dr = sb("dr", [128, 2, 3072], BF16)
        qm = sb("qm", [128, 2, 2, 4 * 128], BF16)
        cs = sb("cs_sb", [128, CS_N], F32)
        cb = sb("cb_sb", [128, B_N], BF16)
        kT = sb("kT", [128, 4, 1792], BF16)
        V = sb("V", [128, 14, 512], BF16)
        g = sb("g", [128, 4, 1312], BF16)
        sq = sb("sq", [128, 2, 512], BF16)
        rstd = sb("rstd", [128, 2, 512], F32)
        dummy = sb("dmy_guard", [128, 2], F32)
        diag = sb("diag", [128, 2, 31 * 128], BF16)
        SCR_BYTES = 55296 + 4096
        scr = sb("scr", [128, SCR_BYTES // 4], F32)
        ps = [es.enter_context(nc.psum_tensor("ps%d" % i, [128, 512], F32)) for i in range(8)]

        def carve(off, nbytes, dt):
            a = scr[:, off // 4:(off + nbytes) // 4]
            if dt is F32:
                return a
            return a.bitcast(dt)

        hT = carve(0, 20480, BF16).rearrange("p (k t) -> p k t", k=8)
        AT = carve(20480, 30720, BF16).rearrange("p (f t) -> p f t", f=12)
        tmpF = carve(51200, 4096, F32).rearrange("p (i t) -> p i t", i=2)
        qT = carve(20480, 10240, BF16).rearrange("p (j t) -> p j t", j=4)
        gtmp = carve(30720, 4096, F32).rearrange("p (i t) -> p i t", i=2)
        cout = carve(0, 16384, BF16).rearrange("p (i k t) -> p i k t", i=2, k=8)
        Pb = carve(30720, 4096, BF16).rearrange("p (i t) -> p i t", i=4)
        rs = carve(34816, 4096, F32).rearrange("p (i t) -> p i t", i=2)
        acc = carve(38912, 8192, F32).rearrange("p (c t) -> p c t", c=4)
        mean_sb = carve(47104, 2048, F32)
        var_sb = carve(49152, 2048, F32)
        accb = carve(51200, 4096, BF16).rearrange("p (i t) -> p i t", i=4)
        accq = carve(55296, 4096, BF16).rearrange("p (i t) -> p i t", i=4)
        stage = V[:, :, :].rearrange("p a b -> p (a b)")[:, 0:2 * CB_N].bitcast(F32)

        ones_b = cb[:, B_ONE:B_ONE + 128]
        onesS = cb[:, B_ONES:B_ONES + 128]
        ones5 = cb[:, B_ONE5:B_ONE5 + 128]
        ident = cb[:, B_ID:B_ID + 128]
        hval = cb[:, B_HV:B_HV + 128]
        mask0 = cb[:, B_M0:B_M0 + 512]

        class Ring:
            def __init__(self, name, buf, nslots, pieces):
                self.name, self.buf, self.n, self.pieces = name, buf, nslots, pieces
                self.issued = 0
                self.consumed = 0

            def issue(self):
                if self.issued >= len(self.pieces):
                    return
                i = self.issued
                self.issued += 1
                slot = i % self.n
                src, ncols = self.pieces[i]
                dst = self.buf[:, slot, 0:ncols]
                P.dma("pool", lambda h, dst=dst, src=src: h.dma_start(out=dst, in_=src),
                      "%s%d" % (self.name, slot), reads=(), writes=((self.name, slot),))

            def take(self):
                i = self.consumed
                self.consumed += 1
                assert i < self.issued
                return i % self.n

            def release(self):
                self.issue()

        gu_pieces = []
        d_pieces = []
        for s_ in range(2):
            gu_pieces += [(wgu_d[0][i], 4096) for i in range(11)]
            gu_pieces += [(win_d[i], 4096) for i in range(5)]
            gu_pieces += [(wout_d[i], 4096) for i in range(2)]
            gu_pieces += [(wgu_d[1][i], 4096) for i in range(11)]
            for f_ in range(2):
                d_pieces += [(wdA_d[f_][i], 3072) for i in range(4)]
                d_pieces += [(wdB_d[f_][i], 2560) for i in range(4)]
        GU = Ring("gu", gur, 3, gu_pieces)
        DR = Ring("d", dr, 2, d_pieces)

        rot = {"sq": 0, "gub": 0, "yb": 0, "tmp": 0, "ev": 0, "st": 0}

        def nxt(k, n):
            v = rot[k] % n
            rot[k] = (v + 1) % n
            return v

        P.dma("sp", lambda h: h.dma_start(out=cs[:, :], in_=cs_d), "cs0", writes=("cs",))
        for e_ in ("act", "dve"):
            P._need(e_, ("cs0", 16, "dma:cs0"))

        def load_x(sbi, subs=(0, 1, 2)):
            for s_ in subs:
                n_ = SUBS[s_][1]
                src = xin_d[sbi, :, XOFF[s_]:XOFF[s_] + 8 * n_]
                keys = tuple(("x", k, s_) for k in range(8))
                P.dma("sp", lambda h, src=src, s_=s_: h.dma_start(out=xblk(s_), in_=src),
                      "xld%d" % s_, writes=keys)

        load_x(0)
        P.dma("sp", lambda h: h.dma_start(out=stage, in_=cbig_d), "cst", writes=("stage",))
        P._need("pool", ("xld0", 16, "dma:xld0"))
        for _ in range(3):
            GU.issue()
        d_deferred = [2]

        P.op("dve", lambda h: h.memset(ones_b, 1.0), writes=("cb1",))
        P.op("dve", lambda h: h.memset(onesS, 1.0 / 1024), writes=("cb1",))
        P.op("dve", lambda h: h.memset(ones5, 1.0 / 512), writes=("cb1",))
        P.op("dve", lambda h: h.memset(g[:, :, 0:32], 0.0), writes=tuple(("g", c) for c in range(4)))
        P.op("dve", lambda h: h.memset(qm[:, :, :, :], 0.0),
             writes=tuple(("qm", i_, p_) for i_ in range(2) for p_ in range(2)))

        def setup_late():
            P.op("dve", lambda h: h.tensor_copy(out=cb[:, B_M0:B_M0 + 768],
                                                in_=stage[:, CB_M0:CB_M0 + 768]),
                 reads=("stage",), writes=("cb",))
            for hh in range(8):
                for t_ in range(2):
                    i_ = hh * 2 + t_
                    src = stage[:, CB_BG + i_ * 128:CB_BG + (i_ + 1) * 128]
                    dst = cb[:, B_BT + i_ * 128:B_BT + (i_ + 1) * 128]
                    rbc = cs[:, CS_RB + hh:CS_RB + hh + 1]
                    if t_ == 0:
                        P.op("dve", lambda h, dst=dst, src=src, rbc=rbc: h.tensor_scalar(
                            out=dst, in0=src, scalar1=rbc, scalar2=None, op0=ALU.subtract),
                            reads=("stage", "cs"), writes=("cb",))
                    else:
                        m4 = stage[:, CB_M4:CB_M4 + 128]
                        P.op("dve", lambda h, dst=dst, src=src, rbc=rbc, m4=m4: h.scalar_tensor_tensor(
                            out=dst, in0=src, scalar=rbc, in1=m4, op0=ALU.subtract, op1=ALU.add),
                            reads=("stage", "cs"), writes=("cb",))
            vk = tuple(("V", i) for i in range(14))
            P.op("dve", lambda h: h.memset(dummy[:, :], 0.0), reads=("stage",), writes=vk)

        def rms_to_h(c0, n, s, norm_idx, dst_fn, dst_keys):
            b = 6 + nxt("st", 2)
            bank = ps[b][:, 0:n]
            for kc in range(8):
                i = nxt("sq", 2)
                sqt = sq[:, i, 0:n]
                xin = xsl(kc, s)
                P.op("act", lambda h, sqt=sqt, xin=xin: h.activation(out=sqt, in_=xin, func=AF.Square),
                     reads=(("x", kc, s),), writes=(("sq", i),))
                P.op("pe", lambda h, bank=bank, sqt=sqt, kc=kc: h.matmul(
                    bank, lhsT=onesS, rhs=sqt, start=(kc == 0), stop=(kc == 7)),
                    reads=(("sq", i), "cb1"), writes=(("ps", b),), inc=True)
            r = nxt("tmp", 2)
            rt = rstd[:, r, 0:n]
            P.op("act", lambda h, rt=rt, bank=bank: h.activation(out=rt, in_=bank, func=AF.Ln, bias=EPS),
                 reads=(("ps", b),), writes=(("rstd", r),))
            P.op("act", lambda h, rt=rt: h.activation(out=rt, in_=rt, func=AF.Exp, scale=-0.5),
                 reads=(("rstd", r),), writes=(("rstd", r),))
            for kc in range(8):
                xin = xsl(kc, s)
                gcol = cs[:, CS_GAIN + norm_idx * 8 + kc:CS_GAIN + norm_idx * 8 + kc + 1]
                dst = dst_fn(kc)
                P.op("dve", lambda h, dst=dst, xin=xin, gcol=gcol, rt=rt: h.scalar_tensor_tensor(
                    out=dst, in0=xin, scalar=gcol, in1=rt, op0=ALU.mult, op1=ALU.mult),
                    reads=(("x", kc, s), ("rstd", r)), writes=dst_keys(kc))

        def norm_h(s, norm_idx):
            c0, n = SUBS[s]
            rms_to_h(c0, n, s, norm_idx, lambda kc: hT[:, kc, c0:c0 + n], lambda kc: (("h", s),))

        def ffn(fi, subs, norm_idx, post=None):
            for s in subs:
                norm_h(s, norm_idx)
            for hf, (f0, nf) in enumerate(HALF):
                for pc in range(nf // 2):
                    slot = GU.take()
                    wv = gur[:, slot, :].rearrange("p (u k c) -> p u k c", u=2, k=8)
                    for fl2 in range(2):
                        fl = pc * 2 + fl2
                        for s in subs:
                            c0, n = SUBS[s]
                            bp = nxt("gub", 2)
                            bG, bU = 2 * bp, 2 * bp + 1
                            for u_, b in ((0, bG), (1, bU)):
                                for kc in range(8):
                                    lhsT = wv[:, u_, kc, fl2 * 128:(fl2 + 1) * 128]
                                    rhs = hT[:, kc, c0:c0 + n]
                                    bank = ps[b][:, 0:n]
                                    P.op("pe", lambda h, bank=bank, lhsT=lhsT, rhs=rhs, kc=kc: h.matmul(
                                        bank, lhsT=lhsT, rhs=rhs, start=(kc == 0), stop=(kc == 7)),
                                        reads=(("gu", slot), ("h", s)), writes=(("ps", b),),
                                        inc=(kc == 7))
                            ti = nxt("tmp", 2)
                            tt = tmpF[:, ti, 0:n]
                            gb = ps[bG][:, 0:n]
                            ub = ps[bU][:, 0:n]
                            P.op("act", lambda h, tt=tt, gb=gb: h.activation(out=tt, in_=gb, func=AF.Silu),
                                 reads=(("ps", bG),), writes=(("tmpF", ti),))
                            at = AT[:, fl, c0:c0 + n]
                            P.op("dve", lambda h, at=at, ub=ub, tt=tt: h.tensor_tensor(
                                out=at, in0=ub, in1=tt, op=ALU.mult),
                                reads=(("ps", bU), ("tmpF", ti)), writes=(("AT", fl, s),))
                    GU.release()
                    while d_deferred[0] > 0:
                        DR.issue()
                        d_deferred[0] -= 1
                for dg in range(4):
                    slot = DR.take()
                    wv = dr[:, slot, 0:nf * 256].rearrange("p (f c) -> p f c", f=nf)
                    lastdg = (post is not None and hf == 1 and dg == 3)
                    order = [(dm2, s) for dm2 in range(2) for s in subs]
                    if lastdg:
                        order = [(dm2, s) for s in subs for dm2 in range(2)]
                    for (dm2, s) in order:
                        dm = dg * 2 + dm2
                        if True:
                            c0, n = SUBS[s]
                            b = 4 + nxt("yb", 2)
                            bank = ps[b][:, 0:n]
                            for fl in range(nf):
                                lhsT = wv[:, fl, dm2 * 128:(dm2 + 1) * 128]
                                rhs = AT[:, fl, c0:c0 + n]
                                P.op("pe", lambda h, bank=bank, lhsT=lhsT, rhs=rhs, fl=fl, nf=nf: h.matmul(
                                    bank, lhsT=lhsT, rhs=rhs, start=(fl == 0), stop=(fl == nf - 1)),
                                    reads=(("d", slot), ("AT", fl, s)), writes=(("ps", b),),
                                    inc=(fl == nf - 1))
                            xo = xsl(dm, s)
                            P.op("dve", lambda h, xo=xo, bank=bank: h.scalar_tensor_tensor(
                                out=xo, in0=bank, scalar=0.5, in1=xo, op0=ALU.mult, op1=ALU.add),
                                reads=(("ps", b), ("x", dm, s)), writes=(("x", dm, s),))
                        if lastdg and dm2 == 1:
                            post(s)
                    DR.release()

        def evac(out, bank, b, wkeys, scale=None):
            i = nxt("ev", 2)
            if i == 0:
                if scale is None:
                    P.op("act", lambda h: h.copy(out=out, in_=bank), reads=(("ps", b),), writes=wkeys)
                else:
                    P.op("act", lambda h: h.mul(out=out, in_=bank, mul=scale), reads=(("ps", b),), writes=wkeys)
            else:
                if scale is None:
                    P.op("dve", lambda h: h.tensor_copy(out=out, in_=bank), reads=(("ps", b),), writes=wkeys)
                else:
                    P.op("dve", lambda h: h.tensor_scalar(out=out, in0=bank, scalar1=scale, scalar2=None,
                                                          op0=ALU.mult), reads=(("ps", b),), writes=wkeys)

        def proj(sbi, subs_all, subs_main, do_norm=True):
            if do_norm:
                for s in subs_all:
                    norm_h(s, 1)
            for which, subs in ((0, subs_main), (1, subs_all)):
                slot = GU.take()
                wv = gur[:, slot, :].rearrange("p (k c) -> p k c", k=8)
                for j in range(4):
                    for s in subs:
                        c0, n = SUBS[s]
                        b = nxt("gub", 4)
                        bank = ps[b][:, 0:n]
                        for kc in range(8):
                            lhsT = wv[:, kc, j * 128:(j + 1) * 128]
                            rhs = hT[:, kc, c0:c0 + n]
                            P.op("pe", lambda h, bank=bank, lhsT=lhsT, rhs=rhs, kc=kc: h.matmul(
                                bank, lhsT=lhsT, rhs=rhs, start=(kc == 0), stop=(kc == 7)),
                                reads=(("gu", slot), ("h", s)), writes=(("ps", b),), inc=(kc == 7))
                        if which == 0:
                            evac(qT[:, j, c0:c0 + n], bank, b, (("q", j, s),), scale=0.125)
                        else:
                            evac(kT[:, j, 512 + c0:512 + c0 + n], bank, b, (("k", j, s + 1),))
                GU.release()
            slot = GU.take()
            wv = gur[:, slot, :].rearrange("p (k c) -> p k c", k=8)
            for s in subs_all:
                c0, n = SUBS[s]
                for tt in range(n // 128):
                    tk = (c0 // 128) + tt
                    b = nxt("gub", 4)
                    bank = ps[b][:, :]
                    for kc in range(8):
                        lhsT = hT[:, kc, tk * 128:(tk + 1) * 128]
                        rhs = wv[:, kc, :]
                        P.op("pe", lambda h, bank=bank, lhsT=lhsT, rhs=rhs, kc=kc: h.matmul(
                            bank, lhsT=lhsT, rhs=rhs, start=(kc == 0), stop=(kc == 7)),
                            reads=(("gu", slot), ("h", s)), writes=(("ps", b),), inc=(kc == 7))
                    evac(V[:, 4 + tk, :], bank, b, (("V", 4 + tk),))
            GU.release()
            for up in range(2):
                slot = GU.take()
                wv = gur[:, slot, :].rearrange("p (k c) -> p k c", k=8)
                for c2 in range(2):
                    c = up * 2 + c2
                    for s in subs_all:
                        c0, n = SUBS[s]
                        if sbi == 0 and s == 0:
                            c0, n = 480, 32
                        bp = nxt("gub", 2)
                        bA, bB = 2 * bp, 2 * bp + 1
                        for ab, b in ((0, bA), (1, bB)):
                            col = (c2 * 2 + ab) * 128
                            bank = ps[b][:, 0:n]
                            for kc in range(8):
                                lhsT = wv[:, kc, col:col + 128]
                                rhs = hT[:, kc, c0:c0 + n]
                                P.op("pe", lambda h, bank=bank, lhsT=lhsT, rhs=rhs, kc=kc: h.matmul(
                                    bank, lhsT=lhsT, rhs=rhs, start=(kc == 0), stop=(kc == 7)),
                                    reads=(("gu", slot), ("h", s)), writes=(("ps", b),), inc=(kc == 7))
                        ti = nxt("tmp", 2)
                        tt = gtmp[:, ti, 0:n]
                        bbank = ps[bB][:, 0:n]
                        abank = ps[bA][:, 0:n]
                        P.op("act", lambda h, tt=tt, bbank=bbank: h.activation(out=tt, in_=bbank, func=AF.Sigmoid),
                             reads=(("ps", bB),), writes=(("gtmp", ti),))
                        gd = g[:, c, 32 + c0:32 + c0 + n]
                        P.op("dve", lambda h, gd=gd, abank=abank, tt=tt: h.tensor_tensor(
                            out=gd, in0=abank, in1=tt, op=ALU.mult),
                            reads=(("ps", bA), ("gtmp", ti)), writes=(("g", c),))
                GU.release()

        PEND = []
        PEND_EVAC = []
        PEND_ACT = []
        HOLD = [False]
        PEND_PE = []

        def flush_pe():
            while PEND_PE:
                PEND_PE.pop(0)()

        def flush_act(force=False):
            while PEND_ACT:
                PEND_ACT.pop(0)()
            if force or not HOLD[0]:
                while PEND_EVAC:
                    PEND_EVAC.pop(0)()

        def attn_steps(sbi, s, ci):
            c0, n = SUBS[s]
            steps = []
            pend = PEND
            for qt in range(n // 128):
                t0 = c0 + qt * 128
                qo = t0 - c0
                st = {"qb": None}

                def setup_q(t0=t0, st=st):
                    qb = nxt("qm", 2)
                    st["qb"] = qb
                    for par in range(2):
                        po_ = par * 64
                        dst = qm[po_:po_ + 64, qb, par, :].rearrange("p (j t) -> p j t", j=4)
                        src = qT[po_:po_ + 64, :, t0:t0 + 128]
                        P.op("act", lambda h, dst=dst, src=src: h.copy(out=dst, in_=src),
                             reads=tuple(("q", j_, s) for j_ in range(4)), writes=(("qm", qb, par),))

                for hg in range(2):
                    gctx = {}

                    def qk(t, t0=t0, hg=hg, st=st, gctx=gctx, first=(hg == 0), setup_q=setup_q):
                        if t == 0 and first:
                            setup_q()
                        if t == 0:
                            ob = 2 + nxt("ob", 2)
                            gctx["ob"] = ob
                            gctx["sumb"] = 4 + (ob - 2)
                        qb = st["qb"]
                        kt = t0 // 128 + t
                        kc0 = kt * 128
                        sbk = nxt("sb", 2)
                        gctx[("sbk", t)] = sbk
                        S = ps[sbk]
                        for hl in range(4):
                            hh = hg + 2 * hl
                            j = hh // 2
                            lhsT = kT[:, j, kc0:kc0 + 128]
                            rhs = qm[:, qb, hg, j * 128:(j + 1) * 128]
                            last = not (t == 0 or t >= 3)
                            kblk = ("k", j, kc0 // 512)
                            P.op("pe", lambda h, o=S[:, hl * 128:(hl + 1) * 128], lhsT=lhsT, rhs=rhs, last=last, hl=hl:
                                 h.matmul(o, lhsT=lhsT, rhs=rhs, start=(hl == 0), stop=(last and hl == 3),
                                          skip_group_check=True),
                                 reads=(kblk, ("qm", qb, hg)), writes=(("ps", sbk),), inc=(last and hl == 3))
                        if t == 0:
                            P.op("pe", lambda h, o=S[:, :]: h.matmul(o, lhsT=ident, rhs=mask0, start=False, stop=True,
                                                                      skip_group_check=True),
                                 reads=("cb",), writes=(("ps", sbk),), inc=True)
                        elif t >= 3:
                            for hl in range(4):
                                hh = hg + 2 * hl
                                bt = cb[:, B_BT + (hh * 2 + t - 3) * 128:B_BT + (hh * 2 + t - 2) * 128]
                                P.op("pe", lambda h, o=S[:, hl * 128:(hl + 1) * 128], bt=bt, hl=hl:
                                     h.matmul(o, lhsT=ident, rhs=bt, start=False, stop=(hl == 3), skip_group_check=True),
                                     reads=("cb",), writes=(("ps", sbk),), inc=(hl == 3))

                    def pv(t, t0=t0, hg=hg, gctx=gctx, qo=qo):
                        ob, sumb = gctx["ob"], gctx["sumb"]
                        obank = ps[ob][:, :]
                        sbank_ = ps[sumb][:, :]
                        kt = t0 // 128 + t
                        sbk = gctx[("sbk", t)]
                        S = ps[sbk]
                        pi = nxt("pb", 4)
                        pt = Pb[:, pi, :]
                        flush_pe()
                        if pend:
                            pend.pop()()
                        P.op("act", lambda h, pt=pt, S=S: h.activation(out=pt, in_=S[:, :], func=AF.Exp),
                             reads=(("ps", sbk),), writes=(("P", pi),))
                        flush_act()
                        for hl in range(4):
                            hh = hg + 2 * hl
                            j = hh // 2
                            lhsT = V[:, kt, j * 128:(j + 1) * 128]
                            P.op("pe", lambda h, o=obank[:, hl * 128:(hl + 1) * 128], lhsT=lhsT,
                                 rhs=pt[:, hl * 128:(hl + 1) * 128], t=t, hl=hl:
                                 h.matmul(o, lhsT=lhsT, rhs=rhs, start=(t == 0 and hl == 0), stop=(t == 4 and hl == 3),
                                          skip_group_check=True),
                                 reads=(("V", kt), ("P", pi)), writes=(("ps", ob),), inc=False)
                        vl = hval if (sbi == 0 and 4 <= kt < 8) else ones_b

                        def sum_mm(vl=vl, pt=pt, t=t, sb_=sbank_, pi=pi, sumb=sumb):
                            P.op("pe", lambda h: h.matmul(sb_, lhsT=vl, rhs=pt, start=(t == 0), stop=(t == 4)),
                                 reads=("cb", "cb1", ("P", pi)), writes=(("ps", sumb),), inc=True)
                        if t == 4:
                            sum_mm()
                        else:
                            pend.append(sum_mm)
                        if t == 4:
                            ri = nxt("rs", 2)
                            rt = rs[:, ri, :]
                            P.op("dve", lambda h, rt=rt, sb_=sbank_: h.reciprocal(out=rt, in_=sb_),
                                 reads=(("ps", sumb),), writes=(("rs", ri),))
                            for hl in range(4):
                                hh = hg + 2 * hl
                                j, po = hh // 2, (hh % 2) * 64
                                o = cout[po:po + 64, ci, j, qo:qo + 128]
                                P.op("dve", lambda h, o=o, a=obank[po:po + 64, hl * 128:(hl + 1) * 128],
                                     b_=rt[po:po + 64, hl * 128:(hl + 1) * 128]:
                                     h.tensor_tensor(out=o, in0=a, in1=b_, op=ALU.mult),
                                     reads=(("ps", ob), ("rs", ri)), writes=(("cout", ci, j),))

                    for t in range(5):
                        steps.append((lambda t=t, qk=qk: qk(t), lambda t=t, pv=pv: pv(t)))
            return steps

        def build_diag(c, di):
            for j in range(31):
                wcol = cs[:, CS_DWK + c * 31 + j:CS_DWK + c * 31 + j + 1]
                dst = diag[:, di, j * 128:(j + 1) * 128]
                if True:
                    P.op("dve", lambda h, dst=dst, wcol=wcol: h.tensor_scalar(
                        out=dst, in0=ident, scalar1=wcol, scalar2=None, op0=ALU.mult),
                        reads=("cb",), writes=(("diag", di),))
                else:
                    P.op("act", lambda h, dst=dst, wcol=wcol: h.activation(
                        out=dst, in_=ident, func=AF.Copy, scale=wcol),
                        reads=("cb",), writes=(("diag", di),))

        def conv_chunk(s, c, di):
            c0, n = SUBS[s]
            if True:
                b = 6 + nxt("st", 2)
                bank = ps[b][:, 0:n]
                for j in range(31):
                    lhsT = diag[:, di, j * 128:(j + 1) * 128]
                    rhs = g[:, c, c0 + 2 + j:c0 + 2 + j + n]
                    P.op("pe", lambda h, bank=bank, lhsT=lhsT, rhs=rhs, j=j: h.matmul(
                        bank, lhsT=lhsT, rhs=rhs, start=(j == 0), stop=(j == 30)),
                        reads=(("diag", di), ("g", c)), writes=(("ps", b),), inc=(j == 30))
                bcol = cs[:, CS_DWB + c:CS_DWB + c + 1]
                a = acc[:, c, 0:n]

                def evac(a=a, bank=bank, bcol=bcol, b=b, c=c):
                    P.op("act", lambda h: h.activation(out=a, in_=bank, func=AF.Identity, bias=bcol),
                         reads=(("ps", b), ("accfree", c)), writes=(("acc", c),))
                PEND_EVAC.append(evac)
        def ln_stages(s, ci):
            c0, n = SUBS[s]
            vs = var_sb[:, 0:n]
            ms = mean_sb[:, 0:n]
            st = {}

            def L1():
                flush_act(force=True)
                HOLD[0] = True
                for c in range(4):
                    a = acc[:, c, 0:n]
                    P.op("act", lambda h, ab=accb[:, c, 0:n], a=a: h.copy(out=ab, in_=a),
                         reads=(("acc", c),), writes=(("accb", c),))
                for c in range(4):
                    a = acc[:, c, 0:n]
                    P.op("act", lambda h, ab=accq[:, c, 0:n], a=a: h.activation(out=ab, in_=a, func=AF.Square),
                         reads=(("acc", c),), writes=(("accq", c),))

                def stats():
                    bm = 6 + nxt("st", 2)
                    for c in range(4):
                        ab = accb[:, c, 0:n]
                        P.op("pe", lambda h, ab=ab, c=c, bm=bm: h.matmul(ps[bm][:, 0:n], lhsT=ones5, rhs=ab,
                                                                         start=(c == 0), stop=(c == 3)),
                             reads=(("accb", c), "cb1"), writes=(("ps", bm),), inc=(c == 3))
                    PEND_ACT.append(lambda bm=bm: P.op(
                        "act", lambda h: h.copy(out=ms, in_=ps[bm][:, 0:n]),
                        reads=(("ps", bm),), writes=("mean",)))
                    bq = 6 + nxt("st", 2)
                    st["bq"] = bq
                    for c in range(4):
                        ab = accq[:, c, 0:n]
                        P.op("pe", lambda h, ab=ab, c=c, bq=bq: h.matmul(ps[bq][:, 0:n], lhsT=ones5, rhs=ab,
                                                                         start=(c == 0), stop=(c == 3)),
                             reads=(("accq", c), "cb1"), writes=(("ps", bq),), inc=(c == 3))
                PEND_PE.append(stats)

            def L2():
                flush_pe()
                bq = st["bq"]
                flush_act()
                P.op("dve", lambda h: h.tensor_tensor(out=vs, in0=ms, in1=ms, op=ALU.mult),
                     reads=("mean",), writes=("var",))
                P.op("dve", lambda h, bq=bq: h.tensor_tensor(out=vs, in0=ps[bq][:, 0:n], in1=vs, op=ALU.subtract),
                     reads=(("ps", bq), "var"), writes=("var",))
                P.op("dve", lambda h: h.tensor_scalar_max(out=vs, in0=vs, scalar1=0.0),
                     reads=("var",), writes=("var",))

            def L2b():
                P.op("act", lambda h: h.activation(out=vs, in_=vs, func=AF.Ln, bias=EPS),
                     reads=("var",), writes=("var",))
                P.op("act", lambda h: h.activation(out=vs, in_=vs, func=AF.Exp, scale=-0.5),
                     reads=("var",), writes=("var",))

            def L3():
                for c in range(4):
                    a = acc[:, c, 0:n]
                    P.op("dve", lambda h, a=a: h.tensor_tensor(out=a, in0=a, in1=ms, op=ALU.subtract),
                         reads=(("acc", c), "mean"), writes=(("acc", c),))
                for c in range(4):
                    a = acc[:, c, 0:n]
                    P.op("dve", lambda h, a=a: h.tensor_tensor(out=a, in0=a, in1=vs, op=ALU.mult),
                         reads=(("acc", c), "var"), writes=(("acc", c),))

            def L4():
                for c in range(4):
                    a = acc[:, c, 0:n]
                    gcol = cs[:, CS_LNG + c:CS_LNG + c + 1]
                    bcol = cs[:, CS_LNB + c:CS_LNB + c + 1]
                    o = cout[:, ci, 4 + c, 0:n]
                    P.op("act", lambda h, a=a, o=o, gcol=gcol, bcol=bcol: h.activation(
                        out=o, in_=a, func=AF.Silu, bias=bcol, scale=gcol),
                        reads=(("acc", c),), writes=(("cout", ci, 4 + c), ("accfree", c)))
                HOLD[0] = False
            return [L1, L2, L2b, L3, L4]

        def wout_sub(s, ci, slots, pcs=(0, 1)):
            c0, n = SUBS[s]
            for pc in pcs:
                slot = slots[pc]
                wv = gur[:, slot, :].rearrange("p (k c) -> p k c", k=8)
                for d4 in range(4):
                    dm = pc * 4 + d4
                    b = 6 + nxt("st", 2)
                    bank = ps[b][:, 0:n]
                    for kc in range(8):
                        lhsT = wv[:, kc, d4 * 128:(d4 + 1) * 128]
                        rhs = cout[:, ci, kc, 0:n]
                        P.op("pe", lambda h, bank=bank, lhsT=lhsT, rhs=rhs, kc=kc: h.matmul(
                            bank, lhsT=lhsT, rhs=rhs, start=(kc == 0), stop=(kc == 7)),
                            reads=(("gu", slot), ("cout", ci, kc)), writes=(("ps", b),), inc=(kc == 7))
                    xo = xsl(dm, s)
                    P.op("dve", lambda h, xo=xo, bank=bank: h.tensor_tensor(out=xo, in0=bank, in1=xo, op=ALU.add),
                         reads=(("ps", b), ("x", dm, s)), writes=(("x", dm, s),))

        rot.update({"ob": 0, "sb": 0, "pb": 0, "rs": 0, "dg": 0, "qm": 0})

        def mix(sbi, subs_main):
            slots = [GU.take(), GU.take()]
            dstate = {}
            plan = []
            for idx, s in enumerate(subs_main):
                ci = idx % 2
                plan.append((s, ci, attn_steps(sbi, s, ci)))
            dstate["di"] = nxt("dg", 2)
            build_diag(0, dstate["di"])
            plan[0][2][0][0]()
            prev = None
            for idx, (s, ci, steps) in enumerate(plan):
                last_sub = (idx == len(plan) - 1)

                def conv_f(c, s=s, last_sub=last_sub):
                    di = dstate["di"]
                    conv_chunk(s, c, di)
                    if c < 3 or not last_sub:
                        dstate["di"] = nxt("dg", 2)
                        build_diag((c + 1) % 4, dstate["di"])
                fillers = [lambda c=c, conv_f=conv_f: conv_f(c) for c in range(4)]
                if prev is not None:
                    ps_, pci = prev
                    fillers.insert(2, lambda ps_=ps_, pci=pci: wout_sub(ps_, pci, slots, (0,)))
                    fillers.append(lambda ps_=ps_, pci=pci: wout_sub(ps_, pci, slots, (1,)))
                    lns = ln_stages(ps_, pci)
                    fillers = lns[0:4] + [fillers[0], lns[4]] + fillers[1:]
                nst = len(steps)
                every = max(1, nst // (len(fillers) + 1))
                fi = 0
                for i in range(nst):
                    if i + 1 < nst:
                        steps[i + 1][0]()
                    elif not last_sub:
                        plan[idx + 1][2][0][0]()
                    steps[i][1]()
                    if (i + 1) % every == 0 and fi < len(fillers):
                        fillers[fi]()
                        fi += 1
                while fi < len(fillers):
                    fillers[fi]()
                    fi += 1
                prev = (s, ci)
            for f_ in ln_stages(prev[0], prev[1]):
                f_()
            flush_act(force=True)
            wout_sub(prev[0], prev[1], slots)
            GU.release()
            GU.release()

        def final_sub(sbi, subs_main, s):
            c0, n = SUBS[s]
            rms_to_h(c0, n, s, 3, lambda kc: xsl(kc, s), lambda kc: (("x", kc, s),))
            m0 = SUBS[subs_main[0]][0]
            o0 = (0 if sbi == 0 else 768) + (c0 - m0)
            dst = yout_d[:, 8 * o0:8 * (o0 + n)]
            keys = tuple(("x", k, s) for k in range(8))
            P.dma("sp", lambda h, dst=dst, s=s: h.dma_start(out=dst, in_=xblk(s)),
                  "ost%d" % s, reads=keys)

        def dump_dbg():
            keys = tuple(("x", k, s) for k in range(8) for s in range(3))
            P.dma("sp", lambda h: h.dma_start(out=dbg_d, in_=xres_f[:, :]), "ost0", reads=keys)

        for sbi in range(2):
            subs_all = [0, 1, 2]
            subs_main = [1, 2] if sbi == 0 else [0, 1, 2]
            if sbi == 1:
                load_x(1, (1, 2))
                P.op("dve", lambda h: h.tensor_copy(out=kT[:, :, 0:512], in_=kT[:, :, 1280:1792]),
                     reads=tuple(("k", j, b_) for j in range(4) for b_ in (2, 3)),
                     writes=tuple(("k", j, 0) for j in range(4)))
                P.op("dve", lambda h: h.tensor_copy(out=V[:, 0:4, :], in_=V[:, 10:14, :]),
                     reads=tuple(("V", i) for i in range(10, 14)), writes=tuple(("V", i) for i in range(4)))
                P.op("dve", lambda h: h.tensor_copy(out=g[:, :, 0:32], in_=g[:, :, 1280:1312]),
                     reads=tuple(("g", c) for c in range(4)), writes=tuple(("g", c) for c in range(4)))
            ffn(0, subs_all, 0, post=lambda s: norm_h(s, 1))
            if debug == "ffn1" and sbi == 0:
                dump_dbg()
                break
            if sbi == 0:
                setup_late()
            proj(sbi, subs_all, subs_main, do_norm=False)
            if sbi == 0:
                load_x(1, (0,))
            if debug == "proj" and sbi == 0:
                dump_dbg()
                break
            P.barrier(("act", "dve"))
            mix_order = [2, 1] if sbi == 0 else [2, 0, 1]
            mix(sbi, mix_order)
            if debug in ("mix", "mixA", "mixC") and sbi == 0:
                dump_dbg()
                break
            ffn(1, mix_order, 2, post=lambda s, sbi=sbi, subs_main=subs_main: final_sub(sbi, subs_main, s))
            if debug == "sb0":
                break
        for nm, c_ in P.dma_cnt.items():
            if nm.startswith("ost"):
                P.E["sp"].items.append(("wait", nm, 16 * c_))

        P.compress()
        sems = {}
        for nm in P.sem_names():
            sems[nm] = es.enter_context(nc.semaphore(nm))
        with nc.Block() as block:
            @block.tensor
            def _(h):
                _replay(h, P.E["pe"], sems)

            @block.scalar
            def _(h):
                _replay(h, P.E["act"], sems)

            @block.vector
            def _(h):
                _replay(h, P.E["dve"], sems)

            @block.gpsimd
            def _(h):
                _replay(h, P.E["pool"], sems)

            @block.sync
            def _(h):
                _replay(h, P.E["sp"], sems)
    return nc


def _prep_shared(inp):
    f = np.float32
    out = {}
    for fi, nm in enumerate(("ffn1", "ffn2")):
        wg = np.asarray(inp[nm + "_gate"], f)[0]
        wu = np.asarray(inp[nm + "_up"], f)[0]
        wd = np.asarray(inp[nm + "_down"], f)[0]
        gu = np.empty((11, 128, 2, 8, 256), f)
        gu[:, :, 0] = wg.reshape(8, 128, 11, 256).transpose(2, 1, 0, 3)
        gu[:, :, 1] = wu.reshape(8, 128, 11, 256).transpose(2, 1, 0, 3)
        out["wgu%d" % fi] = np.ascontiguousarray(gu.reshape(11, 128, 4096))
        w4 = wd.reshape(22, 128, 4, 256)
        out["wdA%d" % fi] = np.ascontiguousarray(w4[0:12].transpose(2, 1, 0, 3).reshape(4, 128, 3072))
        out["wdB%d" % fi] = np.ascontiguousarray(w4[12:22].transpose(2, 1, 0, 3).reshape(4, 128, 2560))
    win = np.asarray(inp["w_in"], f)[0]
    cols = [np.arange(0, 512), np.arange(512, 1024), np.arange(1024, 1536)]
    for up in range(2):
        cc = []
        for c2 in range(2):
            c = up * 2 + c2
            cc.append(np.arange(1536 + c * 128, 1536 + (c + 1) * 128))
            cc.append(np.arange(2048 + c * 128, 2048 + (c + 1) * 128))
        cols.append(np.concatenate(cc))
    wp = np.empty((5, 128, 8, 512), f)
    for i, cl in enumerate(cols):
        wp[i] = win[:, cl].reshape(8, 128, 512).transpose(1, 0, 2)
    out["win"] = np.ascontiguousarray(wp.reshape(5, 128, 4096))
    wo = np.asarray(inp["w_out"], f)[0]
    wop = np.empty((2, 128, 8, 512), f)
    for i in range(2):
        wop[i] = wo[:, i * 512:(i + 1) * 512].reshape(8, 128, 512).transpose(1, 0, 2)
    out["wout"] = np.ascontiguousarray(wop.reshape(2, 128, 4096))

    cs = np.zeros((128, CS_N), f)
    for ni, nm in enumerate(("ffn1_norm", "mix_norm", "ffn2_norm", "final_norm")):
        v = np.asarray(inp[nm], f).reshape(-1)
        cs[:, CS_GAIN + ni * 8:CS_GAIN + ni * 8 + 8] = v.reshape(8, 128).T
    dwk = np.asarray(inp["dw_kernel"], f)[0]
    cs[:, CS_DWK:CS_DWK + 124] = dwk.reshape(31, 4, 128).transpose(2, 1, 0).reshape(128, 124)
    cs[:, CS_DWB:CS_DWB + 4] = np.asarray(inp["dw_bias"], f)[0].reshape(4, 128).T
    cs[:, CS_LNG:CS_LNG + 4] = np.asarray(inp["conv_ln_g"], f)[0].reshape(4, 128).T
    cs[:, CS_LNB:CS_LNB + 4] = np.asarray(inp["conv_ln_b"], f)[0].reshape(4, 128).T
    rb = np.asarray(inp["rel_bias"], f)[0]
    cs[:, CS_RB:CS_RB + 8] = rb[256][None, :]
    out["cs"] = cs

    ki = np.arange(128)[:, None]
    qi = np.arange(128)[None, :]
    idx3 = np.clip(qi - ki + 128, -128, 128) + 128
    idx4 = np.clip(qi - ki, -128, 128) + 128
    cbig = np.zeros((128, CB_N), f)
    for h in range(8):
        cbig[:, CB_BG + (h * 2) * 128:CB_BG + (h * 2 + 1) * 128] = rb[idx3, h]
        cbig[:, CB_BG + (h * 2 + 1) * 128:CB_BG + (h * 2 + 2) * 128] = rb[idx4, h]
    m4 = np.where((ki >= 64) & (qi < 64), NEG, 0.0).astype(f)
    m0 = np.where((ki < 64) & (qi >= 64), NEG, 0.0).astype(f)
    cbig[:, CB_M4:CB_M4 + 128] = m4
    cbig[:, CB_M0:CB_M0 + 512] = np.tile(m0, (1, 4))
    cbig[:, CB_ID:CB_ID + 128] = np.eye(128, dtype=f)
    out["cbig"] = cbig
    return out


def _core_maps(inp):
    shared = _prep_shared(inp)
    x = np.asarray(inp["x"], np.float32)
    maps = []
    for c in range(8):
        b, half = c // 2, c % 2
        s0 = half * 2048
        xl = np.zeros((2 * T, D), np.float32)
        if half == 0:
            xl[512:] = x[b, 0:2048]
        else:
            xl[:] = x[b, s0 - 512:s0 + 2048]
        m = dict(shared)
        xin = np.empty((2, 128, 8 * T), np.float32)
        for sbi in range(2):
            for s_, (c0, n) in enumerate(SUBS):
                blk = xl[sbi * T + c0:sbi * T + c0 + n]
                off = 8 * c0
                xin[sbi, :, off:off + 8 * n] = blk.reshape(n, 8, 128).transpose(2, 1, 0).reshape(128, 8 * n)
        m["xin"] = xin
        cb = shared["cbig"].copy()
        cb[:, CB_HV:CB_HV + 128] = float(half)
        m["cbig"] = cb
        maps.append(m)
    return maps


_NC_CACHE = {}


def kernel(**inputs):
    maps = _core_maps(inputs)
    if "nc" not in _NC_CACHE:
        _NC_CACHE["nc"] = build_nc()
    nc = _NC_CACHE["nc"]
    res = run_bass_kernel_spmd(nc, maps, core_ids=list(range(8)))
    out = np.empty((4, 4096, D), np.float32)
    for c in range(8):
        b, half = c // 2, c % 2
        yo = np.asarray(res.results[c]["yout"])
        o0 = 0
        for n in (512, 256, 512, 512, 256):
            blk = yo[:, 8 * o0:8 * (o0 + n)].reshape(128, 8, n).transpose(2, 1, 0).reshape(n, D)
            out[b, half * 2048 + o0:half * 2048 + o0 + n, :] = blk
            o0 += n
    return out
```

```python
import numpy as np
from contextlib import ExitStack

import concourse.bass as bass
import concourse.mybir as mybir
from concourse.bass_utils import run_bass_kernel_spmd

F32 = mybir.dt.float32
BF16 = mybir.dt.bfloat16
AF = mybir.ActivationFunctionType
ALU = mybir.AluOpType

D = 1024
DFF = 2816
T = 1280
SUBS = [(0, 512), (512, 512), (1024, 256)]
EPS = 1e-6
NEG = -30000.0
HALF = [(0, 12), (12, 10)]

CS_GAIN = 0
CS_DWK = 32
CS_DWB = 156
CS_LNG = 160
CS_LNB = 164
CS_RB = 168
CS_N = 176
CB_BG = 0
CB_M4 = 2048
CB_M0 = 2176
CB_HV = 2688
CB_ID = 2816
CB_N = 2944
B_BT = 0
B_M0 = 2048
B_HV = 2560
B_ID = 2688
B_ONE = 2816
B_ONES = 2944
B_ONE5 = 3072
B_N = 3200


class _Eng:
    def __init__(self, name):
        self.name = name
        self.items = []
        self.count = 0
        self.waited = {}


class Prog:
    ENGS = ("pe", "act", "dve", "pool", "sp")

    def __init__(self):
        self.E = {n: _Eng(n) for n in self.ENGS}
        self.lastw = {}
        self.readers = {}
        self.dma_cnt = {}

    def _need(self, e, tok):
        sem, val, _ = tok
        E = self.E[e]
        if E.waited.get(sem, 0) >= val:
            return
        E.waited[sem] = val
        E.items.append(("wait", sem, val))

    def _deps(self, e, reads, writes):
        for k in reads:
            t = self.lastw.get(k)
            if t is not None:
                if t[2] == e and e == "pe":
                    continue
                self._need(e, t)
        for k in writes:
            t = self.lastw.get(k)
            if t is not None and t[2] != e:
                self._need(e, t)
            for t in self.readers.get(k, {}).values():
                if t[2] != e:
                    self._need(e, t)

    def _record(self, tok, reads, writes):
        for k in reads:
            self.readers.setdefault(k, {})[tok[2]] = tok
        for k in writes:
            self.lastw[k] = tok
            self.readers[k] = {}

    def op(self, e, fn, reads=(), writes=(), inc=True):
        E = self.E[e]
        self._deps(e, reads, writes)
        if inc:
            E.count += 1
            tok = ("e_" + e, E.count, e)
        else:
            tok = ("e_" + e, E.count + 1, e)
        E.items.append(("op", fn, inc))
        self._record(tok, reads, writes)
        return tok

    def dma(self, e, fn, semname, reads=(), writes=()):
        E = self.E[e]
        self._deps(e, reads, writes)
        c = self.dma_cnt.get(semname, 0) + 1
        self.dma_cnt[semname] = c
        tok = (semname, 16 * c, "dma:" + semname)
        E.items.append(("dma", fn, semname))
        self._record(tok, reads, writes)
        return tok

    def barrier(self, engs=("pe", "act", "dve")):
        toks = {e: ("e_" + e, self.E[e].count, e) for e in engs if self.E[e].count > 0}
        for e in engs:
            for f, t in toks.items():
                if f != e:
                    self._need(e, t)

    def sem_names(self):
        s = ["e_" + e for e in self.ENGS]
        s += list(self.dma_cnt.keys())
        return s


def _replay(h, E, sems):
    for it in E.items:
        if it[0] == "wait":
            h.wait_ge(sems[it[1]], it[2])
        elif it[0] == "op":
            ins = it[1](h)
            if it[2]:
                ins.then_inc(sems["e_" + E.name], 1)
        else:
            it[1](h).then_inc(sems[it[2]], 16)


def build_nc(debug=None):
    nc = bass.Bass("TRN2", target_bir_lowering=False)
    P = Prog()
    _NC_CACHE["prog"] = P

    def din(name, shape):
        return nc.dram_tensor(name, list(shape), F32, kind="ExternalInput").ap()

    xin_d = din("xin", [2, 128, 8 * T])
    wgu_d = [din("wgu%d" % i, [11, 128, 4096]) for i in range(2)]
    wdA_d = [din("wdA%d" % i, [4, 128, 3072]) for i in range(2)]
    wdB_d = [din("wdB%d" % i, [4, 128, 2560]) for i in range(2)]
    win_d = din("win", [5, 128, 4096])
    wout_d = din("wout", [2, 128, 4096])
    cs_d = din("cs", [128, CS_N])
    cbig_d = din("cbig", [128, CB_N])
    yout_d = nc.dram_tensor("yout", [128, 8 * 2048], F32, kind="ExternalOutput").ap()
    dbg_d = None
    if debug:
        dbg_d = nc.dram_tensor("dbg", [128, 8 * T], F32, kind="ExternalOutput").ap()

    with ExitStack() as es:
        def sb(name, shape, dt):
            return es.enter_context(nc.sbuf_tensor(name, list(shape), dt))

        xres_f = sb("xres", [128, 8 * T], F32)
        XOFF = [0, 8 * 512, 8 * 1024]

        def xblk(s_):
            n_ = SUBS[s_][1]
            return xres_f[:, XOFF[s_]:XOFF[s_] + 8 * n_]

        def xsl(kc, s_):
            n_ = SUBS[s_][1]
            return xres_f[:, XOFF[s_] + kc * n_:XOFF[s_] + (kc + 1) * n_]
        gur = sb("gur", [128, 3, 4096], BF16)
        dr = sb("dr", [128, 2, 3072], BF16)
        qm = sb("qm", [128, 2, 2, 4 * 128], BF16)
        cs = sb("cs_sb", [128, CS_N], F32)
        cb = sb("cb_sb", [128, B_N], BF16)
        kT = sb("kT", [128, 4, 1792], BF16)
        V = sb("V", [128, 14, 512], BF16)
        g = sb("g", [128, 4, 1312], BF16)
        sq = sb("sq", [128, 2, 512], BF16)
        rstd = sb("rstd", [128, 2, 512], F32)
        dummy = sb("dmy_guard", [128, 2], F32)
        diag = sb("diag", [128, 2, 31 * 128], BF16)
        SCR_BYTES = 55296 + 4096
        scr = sb("scr", [128, SCR_BYTES // 4], F32)
        ps = [es.enter_context(nc.psum_tensor("ps%d" % i, [128, 512], F32)) for i in range(8)]

        def carve(off, nbytes, dt):
            a = scr[:, off // 4:(off + nbytes) // 4]
            if dt is F32:
                return a
            return a.bitcast(dt)

        hT = carve(0, 20480, BF16).rearrange("p (k t) -> p k t", k=8)
        AT = carve(20480, 30720, BF16).rearrange("p (f t) -> p f t", f=12)
        tmpF = carve(51200, 4096, F32).rearrange("p (i t) -> p i t", i=2)
        qT = carve(20480, 10240, BF16).rearrange("p (j t) -> p j t", j=4)
        gtmp = carve(30720, 4096, F32).rearrange("p (i t) -> p i t", i=2)
        cout = carve(0, 16384, BF16).rearrange("p (i k t) -> p i k t", i=2, k=8)
        Pb = carve(30720, 4096, BF16).rearrange("p (i t) -> p i t", i=4)
        rs = carve(34816, 4096, F32).rearrange("p (i t) -> p i t", i=2)
        acc = carve(38912, 8192, F32).rearrange("p (c t) -> p c t", c=4)
        mean_sb = carve(47104, 2048, F32)
        var_sb = carve(49152, 2048, F32)
        accb = carve(51200, 4096, BF16).rearrange("p (i t) -> p i t", i=4)
        accq = carve(55296, 4096, BF16).rearrange("p (i t) -> p i t", i=4)
        stage = V[:, :, :].rearrange("p a b -> p (a b)")[:, 0:2 * CB_N].bitcast(F32)

        ones_b = cb[:, B_ONE:B_ONE + 128]
        onesS = cb[:, B_ONES:B_ONES + 128]
        ones5 = cb[:, B_ONE5:B_ONE5 + 128]
        ident = cb[:, B_ID:B_ID + 128]
        hval = cb[:, B_HV:B_HV + 128]
        mask0 = cb[:, B_M0:B_M0 + 512]

        class Ring:
            def __init__(self, name, buf, nslots, pieces):
                self.name, self.buf, self.n, self.pieces = name, buf, nslots, pieces
                self.issued = 0
                self.consumed = 0

            def issue(self):
                if self.issued >= len(self.pieces):
                    return
                i = self.issued
                self.issued += 1
                slot = i % self.n
                src, ncols = self.pieces[i]
                dst = self.buf[:, slot, 0:ncols]
                P.dma("pool", lambda h, dst=dst, src=src: h.dma_start(out=dst, in_=src),
                      "%s%d" % (self.name, slot), reads=(), writes=((self.name, slot),))

            def take(self):
                i = self.consumed
                self.consumed += 1
                assert i < self.issued
                return i % self.n

            def release(self):
                self.issue()

        gu_pieces = []
        d_pieces = []
        for s_ in range(2):
            gu_pieces += [(wgu_d[0][i], 4096) for i in range(11)]
            gu_pieces += [(win_d[i], 4096) for i in range(5)]
            gu_pieces += [(wout_d[i], 4096) for i in range(2)]
            gu_pieces += [(wgu_d[1][i], 4096) for i in range(11)]
            for f_ in range(2):
                d_pieces += [(wdA_d[f_][i], 3072) for i in range(4)]
                d_pieces += [(wdB_d[f_][i], 2560) for i in range(4)]
        GU = Ring("gu", gur, 3, gu_pieces)
        DR = Ring("d", dr, 2, d_pieces)

        rot = {"sq": 0, "gub": 0, "yb": 0, "tmp": 0, "ev": 0, "st": 0}

        def nxt(k, n):
            v = rot[k] % n
            rot[k] = (v + 1) % n
            return v

        P.dma("sp", lambda h: h.dma_start(out=cs[:, :], in_=cs_d), "cs0", writes=("cs",))
        for e_ in ("act", "dve"):
            P._need(e_, ("cs0", 16, "dma:cs0"))

        def load_x(sbi, subs=(0, 1, 2)):
            for s_ in subs:
                n_ = SUBS[s_][1]
                src = xin_d[sbi, :, XOFF[s_]:XOFF[s_] + 8 * n_]
                keys = tuple(("x", k, s_) for k in range(8))
                P.dma("sp", lambda h, src=src, s_=s_: h.dma_start(out=xblk(s_), in_=src),
                      "xld%d" % s_, writes=keys)

        load_x(0)
        P.dma("sp", lambda h: h.dma_start(out=stage, in_=cbig_d), "cst", writes=("stage",))
        P._need("pool", ("xld0", 16, "dma:xld0"))
        for _ in range(3):
            GU.issue()
        d_deferred = [2]

        P.op("dve", lambda h: h.memset(ones_b, 1.0), writes=("cb1",))
        P.op("dve", lambda h: h.memset(onesS, 1.0 / 1024), writes=("cb1",))
        P.op("dve", lambda h: h.memset(ones5, 1.0 / 512), writes=("cb1",))
        P.op("dve", lambda h: h.memset(g[:, :, 0:32], 0.0), writes=tuple(("g", c) for c in range(4)))
        P.op("dve", lambda h: h.memset(qm[:, :, :, :], 0.0),
             writes=tuple(("qm", i_, p_) for i_ in range(2) for p_ in range(2)))

        def setup_late():
            P.op("dve", lambda h: h.tensor_copy(out=cb[:, B_M0:B_M0 + 768],
                                                in_=stage[:, CB_M0:CB_M0 + 768]),
                 reads=("stage",), writes=("cb",))
            for hh in range(8):
                for t_ in range(2):
                    i_ = hh * 2 + t_
                    src = stage[:, CB_BG + i_ * 128:CB_BG + (i_ + 1) * 128]
                    dst = cb[:, B_BT + i_ * 128:B_BT + (i_ + 1) * 128]
                    rbc = cs[:, CS_RB + hh:CS_RB + hh + 1]
                    if t_ == 0:
                        P.op("dve", lambda h, dst=dst, src=src, rbc=rbc: h.tensor_scalar(
                            out=dst, in0=src, scalar1=rbc, scalar2=None, op0=ALU.subtract),
                            reads=("stage", "cs"), writes=("cb",))
                    else:
                        m4 = stage[:, CB_M4:CB_M4 + 128]
                        P.op("dve", lambda h, dst=dst, src=src, rbc=rbc, m4=m4: h.scalar_tensor_tensor(
                            out=dst, in0=src, scalar=rbc, in1=m4, op0=ALU.subtract, op1=ALU.add),
                            reads=("stage", "cs"), writes=("cb",))
            vk = tuple(("V", i) for i in range(14))
            P.op("dve", lambda h: h.memset(dummy[:, :], 0.0), reads=("stage",), writes=vk)

        def rms_to_h(c0, n, s, norm_idx, dst_fn, dst_keys):
            b = 6 + nxt("st", 2)
            bank = ps[b][:, 0:n]
            for kc in range(8):
                i = nxt("sq", 2)
                sqt = sq[:, i, 0:n]
                xin = xsl(kc, s)
                P.op("act", lambda h, sqt=sqt, xin=xin: h.activation(out=sqt, in_=xin, func=AF.Square),
                     reads=(("x", kc, s),), writes=(("sq", i),))
                P.op("pe", lambda h, bank=bank, sqt=sqt, kc=kc: h.matmul(
                    bank, lhsT=onesS, rhs=sqt, start=(kc == 0), stop=(kc == 7)),
                    reads=(("sq", i), "cb1"), writes=(("ps", b),), inc=True)
            r = nxt("tmp", 2)
            rt = rstd[:, r, 0:n]
            P.op("act", lambda h, rt=rt, bank=bank: h.activation(out=rt, in_=bank, func=AF.Ln, bias=EPS),
                 reads=(("ps", b),), writes=(("rstd", r),))
            P.op("act", lambda h, rt=rt: h.activation(out=rt, in_=rt, func=AF.Exp, scale=-0.5),
                 reads=(("rstd", r),), writes=(("rstd", r),))
            for kc in range(8):
                xin = xsl(kc, s)
                gcol = cs[:, CS_GAIN + norm_idx * 8 + kc:CS_GAIN + norm_idx * 8 + kc + 1]
                dst = dst_fn(kc)
                P.op("dve", lambda h, dst=dst, xin=xin, gcol=gcol, rt=rt: h.scalar_tensor_tensor(
                    out=dst, in0=xin, scalar=gcol, in1=rt, op0=ALU.mult, op1=ALU.mult),
                    reads=(("x", kc, s), ("rstd", r)), writes=dst_keys(kc))

        def norm_h(s, norm_idx):
            c0, n = SUBS[s]
            rms_to_h(c0, n, s, norm_idx, lambda kc: hT[:, kc, c0:c0 + n], lambda kc: (("h", s),))

        def ffn(fi, subs, norm_idx, post=None):
            for s in subs:
                norm_h(s, norm_idx)
            for hf, (f0, nf) in enumerate(HALF):
                for pc in range(nf // 2):
                    slot = GU.take()
                    wv = gur[:, slot, :].rearrange("p (u k c) -> p u k c", u=2, k=8)
                    for s, fl2 in [(s_, f_) for s_ in subs for f_ in range(2)]:
                        fl = pc * 2 + fl2
                        if True:
                            c0, n = SUBS[s]
                            bp = nxt("gub", 2)
                            bG, bU = 2 * bp, 2 * bp + 1
                            for u_, b in ((0, bG), (1, bU)):
                                for kc in range(8):
                                    lhsT = wv[:, u_, kc, fl2 * 128:(fl2 + 1) * 128]
                                    rhs = hT[:, kc, c0:c0 + n]
                                    bank = ps[b][:, 0:n]
                                    P.op("pe", lambda h, bank=bank, lhsT=lhsT, rhs=rhs, kc=kc: h.matmul(
                                        bank, lhsT=lhsT, rhs=rhs, start=(kc == 0), stop=(kc == 7)),
                                        reads=(("gu", slot), ("h", s)), writes=(("ps", b),),
                                        inc=(kc == 7))
                            ti = nxt("tmp", 2)
                            tt = tmpF[:, ti, 0:n]
                            gb = ps[bG][:, 0:n]
                            ub = ps[bU][:, 0:n]
                            P.op("act", lambda h, tt=tt, gb=gb: h.activation(out=tt, in_=gb, func=AF.Silu),
                                 reads=(("ps", bG),), writes=(("tmpF", ti),))
                            at = AT[:, fl, c0:c0 + n]
                            P.op("dve", lambda h, at=at, ub=ub, tt=tt: h.tensor_tensor(
                                out=at, in0=ub, in1=tt, op=ALU.mult),
                                reads=(("ps", bU), ("tmpF", ti)), writes=(("AT", fl, s),))
                    GU.release()
                    while d_deferred[0] > 0:
                        DR.issue()
                        d_deferred[0] -= 1
                for dg in range(4):
                    slot = DR.take()
                    wv = dr[:, slot, 0:nf * 256].rearrange("p (f c) -> p f c", f=nf)
                    lastdg = (post is not None and hf == 1 and dg == 3)
                    order = [(dm2, s) for dm2 in range(2) for s in subs]
                    if lastdg:
                        order = [(dm2, s) for s in subs for dm2 in range(2)]
                    for (dm2, s) in order:
                        dm = dg * 2 + dm2
                        if True:
                            c0, n = SUBS[s]
                            b = 4 + nxt("yb", 2)
                            bank = ps[b][:, 0:n]
                            for fl in range(nf):
                                lhsT = wv[:, fl, dm2 * 128:(dm2 + 1) * 128]
                                rhs = AT[:, fl, c0:c0 + n]
                                P.op("pe", lambda h, bank=bank, lhsT=lhsT, rhs=rhs, fl=fl, nf=nf: h.matmul(
                                    bank, lhsT=lhsT, rhs=rhs, start=(fl == 0), stop=(fl == nf - 1)),
                                    reads=(("d", slot), ("AT", fl, s)), writes=(("ps", b),),
                                    inc=(fl == nf - 1))
                            xo = xsl(dm, s)
                            P.op("dve", lambda h, xo=xo, bank=bank: h.scalar_tensor_tensor(
                                out=xo, in0=bank, scalar=0.5, in1=xo, op0=ALU.mult, op1=ALU.add),
                                reads=(("ps", b), ("x", dm, s)), writes=(("x", dm, s),))
                        if lastdg and dm2 == 1:
                            post(s)
                    DR.release()

        def evac(out, bank, b, wkeys, scale=None):
            i = nxt("ev", 2)
            if i == 0:
                if scale is None:
                    P.op("act", lambda h: h.copy(out=out, in_=bank), reads=(("ps", b),), writes=wkeys)
                else:
                    P.op("act", lambda h: h.mul(out=out, in_=bank, mul=scale), reads=(("ps", b),), writes=wkeys)
            else:
                if scale is None:
                    P.op("dve", lambda h: h.tensor_copy(out=out, in_=bank), reads=(("ps", b),), writes=wkeys)
                else:
                    P.op("dve", lambda h: h.tensor_scalar(out=out, in0=bank, scalar1=scale, scalar2=None,
                                                          op0=ALU.mult), reads=(("ps", b),), writes=wkeys)

        def proj(sbi, subs_all, subs_main, do_norm=True):
            if do_norm:
                for s in subs_all:
                    norm_h(s, 1)
            for which, subs in ((0, subs_main), (1, subs_all)):
                slot = GU.take()
                wv = gur[:, slot, :].rearrange("p (k c) -> p k c", k=8)
                for j in range(4):
                    for s in subs:
                        c0, n = SUBS[s]
                        b = nxt("gub", 4)
                        bank = ps[b][:, 0:n]
                        for kc in range(8):
                            lhsT = wv[:, kc, j * 128:(j + 1) * 128]
                            rhs = hT[:, kc, c0:c0 + n]
                            P.op("pe", lambda h, bank=bank, lhsT=lhsT, rhs=rhs, kc=kc: h.matmul(
                                bank, lhsT=lhsT, rhs=rhs, start=(kc == 0), stop=(kc == 7)),
                                reads=(("gu", slot), ("h", s)), writes=(("ps", b),), inc=(kc == 7))
                        if which == 0:
                            evac(qT[:, j, c0:c0 + n], bank, b, (("q", j, s),), scale=0.125)
                        else:
                            evac(kT[:, j, 512 + c0:512 + c0 + n], bank, b, (("k", j, s + 1),))
                GU.release()
            slot = GU.take()
            wv = gur[:, slot, :].rearrange("p (k c) -> p k c", k=8)
            for s in subs_all:
                c0, n = SUBS[s]
                for tt in range(n // 128):
                    tk = (c0 // 128) + tt
                    b = nxt("gub", 4)
                    bank = ps[b][:, :]
                    for kc in range(8):
                        lhsT = hT[:, kc, tk * 128:(tk + 1) * 128]
                        rhs = wv[:, kc, :]
                        P.op("pe", lambda h, bank=bank, lhsT=lhsT, rhs=rhs, kc=kc: h.matmul(
                            bank, lhsT=lhsT, rhs=rhs, start=(kc == 0), stop=(kc == 7)),
                            reads=(("gu", slot), ("h", s)), writes=(("ps", b),), inc=(kc == 7))
                    evac(V[:, 4 + tk, :], bank, b, (("V", 4 + tk),))
            GU.release()
            for up in range(2):
                slot = GU.take()
                wv = gur[:, slot, :].rearrange("p (k c) -> p k c", k=8)
                for c2 in range(2):
                    c = up * 2 + c2
                    for s in subs_all:
                        c0, n = SUBS[s]
                        if sbi == 0 and s == 0:
                            c0, n = 480, 32
                        bp = nxt("gub", 2)
                        bA, bB = 2 * bp, 2 * bp + 1
                        for ab, b in ((0, bA), (1, bB)):
                            col = (c2 * 2 + ab) * 128
                            bank = ps[b][:, 0:n]
                            for kc in range(8):
                                lhsT = wv[:, kc, col:col + 128]
                                rhs = hT[:, kc, c0:c0 + n]
                                P.op("pe", lambda h, bank=bank, lhsT=lhsT, rhs=rhs, kc=kc: h.matmul(
                                    bank, lhsT=lhsT, rhs=rhs, start=(kc == 0), stop=(kc == 7)),
                                    reads=(("gu", slot), ("h", s)), writes=(("ps", b),), inc=(kc == 7))
                        ti = nxt("tmp", 2)
                        tt = gtmp[:, ti, 0:n]
                        bbank = ps[bB][:, 0:n]
                        abank = ps[bA][:, 0:n]
                        P.op("act", lambda h, tt=tt, bbank=bbank: h.activation(out=tt, in_=bbank, func=AF.Sigmoid),
                             reads=(("ps", bB),), writes=(("gtmp", ti),))
                        gd = g[:, c, 32 + c0:32 + c0 + n]
                        P.op("dve", lambda h, gd=gd, abank=abank, tt=tt: h.tensor_tensor(
                            out=gd, in0=abank, in1=tt, op=ALU.mult),
                            reads=(("ps", bA), ("gtmp", ti)), writes=(("g", c),))
                GU.release()

        PEND = []
        PEND_EVAC = []
        PEND_ACT = []
        HOLD = [False]
        PEND_PE = []

        def flush_pe():
            while PEND_PE:
                PEND_PE.pop(0)()

        def flush_act(force=False):
            while PEND_ACT:
                PEND_ACT.pop(0)()
            if force or not HOLD[0]:
                while PEND_EVAC:
                    PEND_EVAC.pop(0)()

        def attn_steps(sbi, s, ci):
            c0, n = SUBS[s]
            steps = []
            pend = PEND
            for qt in range(n // 128):
                t0 = c0 + qt * 128
                qo = t0 - c0
                st = {"qb": None}

                def setup_q(t0=t0, st=st):
                    qb = nxt("qm", 2)
                    st["qb"] = qb
                    for par in range(2):
                        po_ = par * 64
                        dst = qm[po_:po_ + 64, qb, par, :].rearrange("p (j t) -> p j t", j=4)
                        src = qT[po_:po_ + 64, :, t0:t0 + 128]
                        P.op("act", lambda h, dst=dst, src=src: h.copy(out=dst, in_=src),
                             reads=tuple(("q", j_, s) for j_ in range(4)), writes=(("qm", qb, par),))

                for hg in range(2):
                    gctx = {}

                    def qk(t, t0=t0, hg=hg, st=st, gctx=gctx, first=(hg == 0), setup_q=setup_q):
                        if t == 0 and first:
                            setup_q()
                        if t == 0:
                            ob = 2 + nxt("ob", 2)
                            gctx["ob"] = ob
                            gctx["sumb"] = 4 + (ob - 2)
                        qb = st["qb"]
                        kt = t0 // 128 + t
                        kc0 = kt * 128
                        sbk = nxt("sb", 2)
                        gctx[("sbk", t)] = sbk
                        S = ps[sbk]
                        for hl in range(4):
                            hh = hg + 2 * hl
                            j = hh // 2
                            lhsT = kT[:, j, kc0:kc0 + 128]
                            rhs = qm[:, qb, hg, j * 128:(j + 1) * 128]
                            last = not (t == 0 or t >= 3)
                            kblk = ("k", j, kc0 // 512)
                            P.op("pe", lambda h, o=S[:, hl * 128:(hl + 1) * 128], lhsT=lhsT, rhs=rhs, last=last, hl=hl:
                                 h.matmul(o, lhsT=lhsT, rhs=rhs, start=(hl == 0), stop=(last and hl == 3),
                                          skip_group_check=True),
                                 reads=(kblk, ("qm", qb, hg)), writes=(("ps", sbk),), inc=(last and hl == 3))
                        if t == 0:
                            P.op("pe", lambda h, o=S[:, :]: h.matmul(o, lhsT=ident, rhs=mask0, start=False, stop=True,
                                                                      skip_group_check=True),
                                 reads=("cb",), writes=(("ps", sbk),), inc=True)
                        elif t >= 3:
                            for hl in range(4):
                                hh = hg + 2 * hl
                                bt = cb[:, B_BT + (hh * 2 + t - 3) * 128:B_BT + (hh * 2 + t - 2) * 128]
                                P.op("pe", lambda h, o=S[:, hl * 128:(hl + 1) * 128], bt=bt, hl=hl:
                                     h.matmul(o, lhsT=ident, rhs=bt, start=False, stop=(hl == 3), skip_group_check=True),
                                     reads=("cb",), writes=(("ps", sbk),), inc=(hl == 3))

                    def pv(t, t0=t0, hg=hg, gctx=gctx, qo=qo):
                        ob, sumb = gctx["ob"], gctx["sumb"]
                        obank = ps[ob][:, :]
                        sbank_ = ps[sumb][:, :]
                        kt = t0 // 128 + t
                        sbk = gctx[("sbk", t)]
                        S = ps[sbk]
                        pi = nxt("pb", 4)
                        pt = Pb[:, pi, :]
                        flush_pe()
                        if pend:
                            pend.pop()()
                        P.op("act", lambda h, pt=pt, S=S: h.activation(out=pt, in_=S[:, :], func=AF.Exp),
                             reads=(("ps", sbk),), writes=(("P", pi),))
                        flush_act()
                        for hl in range(4):
                            hh = hg + 2 * hl
                            j = hh // 2
                            lhsT = V[:, kt, j * 128:(j + 1) * 128]
                            P.op("pe", lambda h, o=obank[:, hl * 128:(hl + 1) * 128], lhsT=lhsT,
                                 rhs=pt[:, hl * 128:(hl + 1) * 128], t=t, hl=hl:
                                 h.matmul(o, lhsT=lhsT, rhs=rhs, start=(t == 0 and hl == 0), stop=(t == 4 and hl == 3),
                                          skip_group_check=True),
                                 reads=(("V", kt), ("P", pi)), writes=(("ps", ob),), inc=False)
                        vl = hval if (sbi == 0 and 4 <= kt < 8) else ones_b

                        def sum_mm(vl=vl, pt=pt, t=t, sb_=sbank_, pi=pi, sumb=sumb):
                            P.op("pe", lambda h: h.matmul(sb_, lhsT=vl, rhs=pt, start=(t == 0), stop=(t == 4)),
                                 reads=("cb", "cb1", ("P", pi)), writes=(("ps", sumb),), inc=True)
                        if t == 4:
                            sum_mm()
                        else:
                            pend.append(sum_mm)
                        if t == 4:
                            ri = nxt("rs", 2)
                            rt = rs[:, ri, :]
                            P.op("dve", lambda h, rt=rt, sb_=sbank_: h.reciprocal(out=rt, in_=sb_),
                                 reads=(("ps", sumb),), writes=(("rs", ri),))
                            for hl in range(4):
                                hh = hg + 2 * hl
                                j, po = hh // 2, (hh % 2) * 64
                                o = cout[po:po + 64, ci, j, qo:qo + 128]
                                P.op("dve", lambda h, o=o, a=obank[po:po + 64, hl * 128:(hl + 1) * 128],
                                     b_=rt[po:po + 64, hl * 128:(hl + 1) * 128]:
                                     h.tensor_tensor(out=o, in0=a, in1=b_, op=ALU.mult),
                                     reads=(("ps", ob), ("rs", ri)), writes=(("cout", ci, j),))

                    for t in range(5):
                        steps.append((lambda t=t, qk=qk: qk(t), lambda t=t, pv=pv: pv(t)))
            return steps

        def build_diag(c, di):
            for j in range(31):
                wcol = cs[:, CS_DWK + c * 31 + j:CS_DWK + c * 31 + j + 1]
                dst = diag[:, di, j * 128:(j + 1) * 128]
                if True:
                    P.op("dve", lambda h, dst=dst, wcol=wcol: h.tensor_scalar(
                        out=dst, in0=ident, scalar1=wcol, scalar2=None, op0=ALU.mult),
                        reads=("cb",), writes=(("diag", di),))
                else:
                    P.op("act", lambda h, dst=dst, wcol=wcol: h.activation(
                        out=dst, in_=ident, func=AF.Copy, scale=wcol),
                        reads=("cb",), writes=(("diag", di),))

        def conv_chunk(s, c, di):
            c0, n = SUBS[s]
            if True:
                b = 6 + nxt("st", 2)
                bank = ps[b][:, 0:n]
                for j in range(31):
                    lhsT = diag[:, di, j * 128:(j + 1) * 128]
                    rhs = g[:, c, c0 + 2 + j:c0 + 2 + j + n]
                    P.op("pe", lambda h, bank=bank, lhsT=lhsT, rhs=rhs, j=j: h.matmul(
                        bank, lhsT=lhsT, rhs=rhs, start=(j == 0), stop=(j == 30)),
                        reads=(("diag", di), ("g", c)), writes=(("ps", b),), inc=(j == 30))
                bcol = cs[:, CS_DWB + c:CS_DWB + c + 1]
                a = acc[:, c, 0:n]

                def evac(a=a, bank=bank, bcol=bcol, b=b, c=c):
                    P.op("act", lambda h: h.activation(out=a, in_=bank, func=AF.Identity, bias=bcol),
                         reads=(("ps", b), ("accfree", c)), writes=(("acc", c),))
                PEND_EVAC.append(evac)
        def ln_stages(s, ci):
            c0, n = SUBS[s]
            vs = var_sb[:, 0:n]
            ms = mean_sb[:, 0:n]
            st = {}

            def L1():
                flush_act(force=True)
                HOLD[0] = True
                for c in range(4):
                    a = acc[:, c, 0:n]
                    P.op("act", lambda h, ab=accb[:, c, 0:n], a=a: h.copy(out=ab, in_=a),
                         reads=(("acc", c),), writes=(("accb", c),))
                for c in range(4):
                    a = acc[:, c, 0:n]
                    P.op("act", lambda h, ab=accq[:, c, 0:n], a=a: h.activation(out=ab, in_=a, func=AF.Square),
                         reads=(("acc", c),), writes=(("accq", c),))

                def stats():
                    bm = 6 + nxt("st", 2)
                    for c in range(4):
                        ab = accb[:, c, 0:n]
                        P.op("pe", lambda h, ab=ab, c=c, bm=bm: h.matmul(ps[bm][:, 0:n], lhsT=ones5, rhs=ab,
                                                                         start=(c == 0), stop=(c == 3)),
                             reads=(("accb", c), "cb1"), writes=(("ps", bm),), inc=(c == 3))
                    PEND_ACT.append(lambda bm=bm: P.op(
                        "act", lambda h: h.copy(out=ms, in_=ps[bm][:, 0:n]),
                        reads=(("ps", bm),), writes=("mean",)))
                    bq = 6 + nxt("st", 2)
                    st["bq"] = bq
                    for c in range(4):
                        ab = accq[:, c, 0:n]
                        P.op("pe", lambda h, ab=ab, c=c, bq=bq: h.matmul(ps[bq][:, 0:n], lhsT=ones5, rhs=ab,
                                                                         start=(c == 0), stop=(c == 3)),
                             reads=(("accq", c), "cb1"), writes=(("ps", bq),), inc=(c == 3))
                PEND_PE.append(stats)

            def L2():
                flush_pe()
                bq = st["bq"]
                flush_act()
                P.op("dve", lambda h: h.tensor_tensor(out=vs, in0=ms, in1=ms, op=ALU.mult),
                     reads=("mean",), writes=("var",))
                P.op("dve", lambda h, bq=bq: h.tensor_tensor(out=vs, in0=ps[bq][:, 0:n], in1=vs, op=ALU.subtract),
                     reads=(("ps", bq), "var"), writes=("var",))
                P.op("dve", lambda h: h.tensor_scalar_max(out=vs, in0=vs, scalar1=0.0),
                     reads=("var",), writes=("var",))

            def L2b():
                P.op("act", lambda h: h.activation(out=vs, in_=vs, func=AF.Ln, bias=EPS),
                     reads=("var",), writes=("var",))
                P.op("act", lambda h: h.activation(out=vs, in_=vs, func=AF.Exp, scale=-0.5),
                     reads=("var",), writes=("var",))

            def L3():
                for c in range(4):
                    a = acc[:, c, 0:n]
                    P.op("dve", lambda h, a=a: h.tensor_tensor(out=a, in0=a, in1=ms, op=ALU.subtract),
                         reads=(("acc", c), "mean"), writes=(("acc", c),))
                for c in range(4):
                    a = acc[:, c, 0:n]
                    P.op("dve", lambda h, a=a: h.tensor_tensor(out=a, in0=a, in1=vs, op=ALU.mult),
                         reads=(("acc", c), "var"), writes=(("acc", c),))

            def L4():
                for c in range(4):
                    a = acc[:, c, 0:n]
                    gcol = cs[:, CS_LNG + c:CS_LNG + c + 1]
                    bcol = cs[:, CS_LNB + c:CS_LNB + c + 1]
                    o = cout[:, ci, 4 + c, 0:n]
                    P.op("act", lambda h, a=a, o=o, gcol=gcol, bcol=bcol: h.activation(
                        out=o, in_=a, func=AF.Silu, bias=bcol, scale=gcol),
                        reads=(("acc", c),), writes=(("cout", ci, 4 + c), ("accfree", c)))
                HOLD[0] = False
            return [L1, L2, L2b, L3, L4]

        def wout_sub(s, ci, slots, pcs=(0, 1)):
            c0, n = SUBS[s]
            for pc in pcs:
                slot = slots[pc]
                wv = gur[:, slot, :].rearrange("p (k c) -> p k c", k=8)
                for d4 in range(4):
                    dm = pc * 4 + d4
                    b = 6 + nxt("st", 2)
                    bank = ps[b][:, 0:n]
                    for kc in range(8):
                        lhsT = wv[:, kc, d4 * 128:(d4 + 1) * 128]
                        rhs = cout[:, ci, kc, 0:n]
                        P.op("pe", lambda h, bank=bank, lhsT=lhsT, rhs=rhs, kc=kc: h.matmul(
                            bank, lhsT=lhsT, rhs=rhs, start=(kc == 0), stop=(kc == 7)),
                            reads=(("gu", slot), ("cout", ci, kc)), writes=(("ps", b),), inc=(kc == 7))
                    xo = xsl(dm, s)
                    P.op("dve", lambda h, xo=xo, bank=bank: h.tensor_tensor(out=xo, in0=bank, in1=xo, op=ALU.add),
                         reads=(("ps", b), ("x", dm, s)), writes=(("x", dm, s),))

        rot.update({"ob": 0, "sb": 0, "pb": 0, "rs": 0, "dg": 0, "qm": 0})

        def mix(sbi, subs_main):
            slots = [GU.take(), GU.take()]
            dstate = {}
            plan = []
            for idx, s in enumerate(subs_main):
                ci = idx % 2
                plan.append((s, ci, attn_steps(sbi, s, ci)))
            dstate["di"] = nxt("dg", 2)
            build_diag(0, dstate["di"])
            plan[0][2][0][0]()
            prev = None
            for idx, (s, ci, steps) in enumerate(plan):
                last_sub = (idx == len(plan) - 1)

                def conv_f(c, s=s, last_sub=last_sub):
                    di = dstate["di"]
                    conv_chunk(s, c, di)
                    if c < 3 or not last_sub:
                        dstate["di"] = nxt("dg", 2)
                        build_diag((c + 1) % 4, dstate["di"])
                fillers = [lambda c=c, conv_f=conv_f: conv_f(c) for c in range(4)]
                if prev is not None:
                    ps_, pci = prev
                    fillers.insert(2, lambda ps_=ps_, pci=pci: wout_sub(ps_, pci, slots, (0,)))
                    fillers.append(lambda ps_=ps_, pci=pci: wout_sub(ps_, pci, slots, (1,)))
                    lns = ln_stages(ps_, pci)
                    fillers = lns[0:4] + [fillers[0], lns[4]] + fillers[1:]
                nst = len(steps)
                every = max(1, nst // (len(fillers) + 1))
                fi = 0
                for i in range(nst):
                    if i + 1 < nst:
                        steps[i + 1][0]()
                    elif not last_sub:
                        plan[idx + 1][2][0][0]()
                    steps[i][1]()
                    if (i + 1) % every == 0 and fi < len(fillers):
                        fillers[fi]()
                        fi += 1
                while fi < len(fillers):
                    fillers[fi]()
                    fi += 1
                prev = (s, ci)
            for f_ in ln_stages(prev[0], prev[1]):
                f_()
            flush_act(force=True)
            wout_sub(prev[0], prev[1], slots)
            GU.release()
            GU.release()

        def final_sub(sbi, subs_main, s):
            c0, n = SUBS[s]
            rms_to_h(c0, n, s, 3, lambda kc: xsl(kc, s), lambda kc: (("x", kc, s),))
            m0 = SUBS[subs_main[0]][0]
            o0 = (0 if sbi == 0 else 768) + (c0 - m0)
            dst = yout_d[:, 8 * o0:8 * (o0 + n)]
            keys = tuple(("x", k, s) for k in range(8))
            P.dma("sp", lambda h, dst=dst, s=s: h.dma_start(out=dst, in_=xblk(s)),
                  "ost%d" % s, reads=keys)

        def dump_dbg():
            keys = tuple(("x", k, s) for k in range(8) for s in range(3))
            P.dma("sp", lambda h: h.dma_start(out=dbg_d, in_=xres_f[:, :]), "ost0", reads=keys)

        for sbi in range(2):
            subs_all = [0, 1, 2]
            subs_main = [1, 2] if sbi == 0 else [0, 1, 2]
            if sbi == 1:
                P.op("dve", lambda h: h.tensor_copy(out=kT[:, :, 0:512], in_=kT[:, :, 1280:1792]),
                     reads=tuple(("k", j, b_) for j in range(4) for b_ in (2, 3)),
                     writes=tuple(("k", j, 0) for j in range(4)))
                P.op("dve", lambda h: h.tensor_copy(out=V[:, 0:4, :], in_=V[:, 10:14, :]),
                     reads=tuple(("V", i) for i in range(10, 14)), writes=tuple(("V", i) for i in range(4)))
                P.op("dve", lambda h: h.tensor_copy(out=g[:, :, 0:32], in_=g[:, :, 1280:1312]),
                     reads=tuple(("g", c) for c in range(4)), writes=tuple(("g", c) for c in range(4)))
            ffn(0, subs_all, 0, post=lambda s: norm_h(s, 1))
            if debug == "ffn1" and sbi == 0:
                dump_dbg()
                break
            if sbi == 0:
                setup_late()
            proj(sbi, subs_all, subs_main, do_norm=False)
            if sbi == 0:
                load_x(1, (0,))
            if debug == "proj" and sbi == 0:
                dump_dbg()
                break
            P.barrier(("act", "dve"))
            mix(sbi, subs_main)
            if debug in ("mix", "mixA", "mixC") and sbi == 0:
                dump_dbg()
                break
            def post2(s, sbi=sbi, subs_main=subs_main):
                final_sub(sbi, subs_main, s)
                if sbi == 0:
                    load_x(1, (s,))
            ffn(1, subs_main, 2, post=post2)
            if debug == "sb0":
                break
        for nm, c_ in P.dma_cnt.items():
            if nm.startswith("ost"):
                P.E["sp"].items.append(("wait", nm, 16 * c_))

        sems = {}
        for nm in P.sem_names():
            sems[nm] = es.enter_context(nc.semaphore(nm))
        with nc.Block() as block:
            @block.tensor
            def _(h):
                _replay(h, P.E["pe"], sems)

            @block.scalar
            def _(h):
                _replay(h, P.E["act"], sems)

            @block.vector
            def _(h):
                _replay(h, P.E["dve"], sems)

            @block.gpsimd
            def _(h):
                _replay(h, P.E["pool"], sems)

            @block.sync
            def _(h):
                _replay(h, P.E["sp"], sems)
    return nc


def _prep_shared(inp):
    f = np.float32
    out = {}
    for fi, nm in enumerate(("ffn1", "ffn2")):
        wg = np.asarray(inp[nm + "_gate"], f)[0]
        wu = np.asarray(inp[nm + "_up"], f)[0]
        wd = np.asarray(inp[nm + "_down"], f)[0]
        gu = np.empty((11, 128, 2, 8, 256), f)
        gu[:, :, 0] = wg.reshape(8, 128, 11, 256).transpose(2, 1, 0, 3)
        gu[:, :, 1] = wu.reshape(8, 128, 11, 256).transpose(2, 1, 0, 3)
        out["wgu%d" % fi] = np.ascontiguousarray(gu.reshape(11, 128, 4096))
        w4 = wd.reshape(22, 128, 4, 256)
        out["wdA%d" % fi] = np.ascontiguousarray(w4[0:12].transpose(2, 1, 0, 3).reshape(4, 128, 3072))
        out["wdB%d" % fi] = np.ascontiguousarray(w4[12:22].transpose(2, 1, 0, 3).reshape(4, 128, 2560))
    win = np.asarray(inp["w_in"], f)[0]
    cols = [np.arange(0, 512), np.arange(512, 1024), np.arange(1024, 1536)]
    for up in range(2):
        cc = []
        for c2 in range(2):
            c = up * 2 + c2
            cc.append(np.arange(1536 + c * 128, 1536 + (c + 1) * 128))
            cc.append(np.arange(2048 + c * 128, 2048 + (c + 1) * 128))
        cols.append(np.concatenate(cc))
    wp = np.empty((5, 128, 8, 512), f)
    for i, cl in enumerate(cols):
        wp[i] = win[:, cl].reshape(8, 128, 512).transpose(1, 0, 2)
    out["win"] = np.ascontiguousarray(wp.reshape(5, 128, 4096))
    wo = np.asarray(inp["w_out"], f)[0]
    wop = np.empty((2, 128, 8, 512), f)
    for i in range(2):
        wop[i] = wo[:, i * 512:(i + 1) * 512].reshape(8, 128, 512).transpose(1, 0, 2)
    out["wout"] = np.ascontiguousarray(wop.reshape(2, 128, 4096))

    cs = np.zeros((128, CS_N), f)
    for ni, nm in enumerate(("ffn1_norm", "mix_norm", "ffn2_norm", "final_norm")):
        v = np.asarray(inp[nm], f).reshape(-1)
        cs[:, CS_GAIN + ni * 8:CS_GAIN + ni * 8 + 8] = v.reshape(8, 128).T
    dwk = np.asarray(inp["dw_kernel"], f)[0]
    cs[:, CS_DWK:CS_DWK + 124] = dwk.reshape(31, 4, 128).transpose(2, 1, 0).reshape(128, 124)
    cs[:, CS_DWB:CS_DWB + 4] = np.asarray(inp["dw_bias"], f)[0].reshape(4, 128).T
    cs[:, CS_LNG:CS_LNG + 4] = np.asarray(inp["conv_ln_g"], f)[0].reshape(4, 128).T
    cs[:, CS_LNB:CS_LNB + 4] = np.asarray(inp["conv_ln_b"], f)[0].reshape(4, 128).T
    rb = np.asarray(inp["rel_bias"], f)[0]
    cs[:, CS_RB:CS_RB + 8] = rb[256][None, :]
    out["cs"] = cs

    ki = np.arange(128)[:, None]
    qi = np.arange(128)[None, :]
    idx3 = np.clip(qi - ki + 128, -128, 128) + 128
    idx4 = np.clip(qi - ki, -128, 128) + 128
    cbig = np.zeros((128, CB_N), f)
    for h in range(8):
        cbig[:, CB_BG + (h * 2) * 128:CB_BG + (h * 2 + 1) * 128] = rb[idx3, h]
        cbig[:, CB_BG + (h * 2 + 1) * 128:CB_BG + (h * 2 + 2) * 128] = rb[idx4, h]
    m4 = np.where((ki >= 64) & (qi < 64), NEG, 0.0).astype(f)
    m0 = np.where((ki < 64) & (qi >= 64), NEG, 0.0).astype(f)
    cbig[:, CB_M4:CB_M4 + 128] = m4
    cbig[:, CB_M0:CB_M0 + 512] = np.tile(m0, (1, 4))
    cbig[:, CB_ID:CB_ID + 128] = np.eye(128, dtype=f)
    out["cbig"] = cbig
    return out


def _core_maps(inp):
    shared = _prep_shared(inp)
    x = np.asarray(inp["x"], np.float32)
    maps = []
    for c in range(8):
        b, half = c // 2, c % 2
        s0 = half * 2048
        xl = np.zeros((2 * T, D), np.float32)
        if half == 0:
            xl[512:] = x[b, 0:2048]
        else:
            xl[:] = x[b, s0 - 512:s0 + 2048]
        m = dict(shared)
        xin = np.empty((2, 128, 8 * T), np.float32)
        for sbi in range(2):
            for s_, (c0, n) in enumerate(SUBS):
                blk = xl[sbi * T + c0:sbi * T + c0 + n]
                off = 8 * c0
                xin[sbi, :, off:off + 8 * n] = blk.reshape(n, 8, 128).transpose(2, 1, 0).reshape(128, 8 * n)
        m["xin"] = xin
        cb = shared["cbig"].copy()
        cb[:, CB_HV:CB_HV + 128] = float(half)
        m["cbig"] = cb
        maps.append(m)
    return maps


_NC_CACHE = {}


def kernel(**inputs):
    maps = _core_maps(inputs)
    if "nc" not in _NC_CACHE:
        _NC_CACHE["nc"] = build_nc()
    nc = _NC_CACHE["nc"]
    res = run_bass_kernel_spmd(nc, maps, core_ids=list(range(8)))
    out = np.empty((4, 4096, D), np.float32)
    for c in range(8):
        b, half = c // 2, c % 2
        yo = np.asarray(res.results[c]["yout"])
        o0 = 0
        for n in (512, 256, 512, 512, 256):
            blk = yo[:, 8 * o0:8 * (o0 + n)].reshape(128, 8, n).transpose(2, 1, 0).reshape(n, D)
            out[b, half * 2048 + o0:half * 2048 + o0 + n, :] = blk
            o0 += n
    return out
```
